# Optimizing a Trainium2 kernel written in Bass

```python
import jax, jax.numpy as jnp
from jax import lax
import numpy as np

D_MODEL = 2048
BATCH = 4
SEQ = 2048
DEPTH = 1
DEC_BATCH = 128
DEC_SEQ = 1
PAST_LEN = 2048
PAGE_SIZE = 128

N_HEADS = 8
HEAD_DIM = 128
ATTN_WIDTH = N_HEADS * HEAD_DIM
Q_BLOCK = 128
D_RNN = 1024
N_RNN_BLOCKS = 8
RNN_BLOCK = D_RNN // N_RNN_BLOCKS
CONV_WIDTH = 4
LRU_C = 8.0
D_FF = ((8 * D_MODEL + 3 * 256 - 1) // (3 * 256)) * 256
IN_COLS = 3 * ATTN_WIDTH + N_HEADS + 2 * D_RNN + 2 * D_MODEL
EPS = 1e-6

kernel_name = 'hybrid_fox_rglru_decode_step'


def rms_norm(x, g):
    xf = x.astype(jnp.float32)
    inv = lax.rsqrt(jnp.mean(xf * xf, axis=-1, keepdims=True) + EPS)
    return (xf * inv).astype(x.dtype) * g


def split_columns(z):
    sizes = (ATTN_WIDTH, ATTN_WIDTH, ATTN_WIDTH, N_HEADS, D_RNN, D_RNN, D_MODEL, D_MODEL)
    idx = np.cumsum(sizes)[:-1].tolist()
    return jnp.split(z, idx, axis=-1)


def fox_scores(q, k, c_q, c_k):
    s = jnp.einsum('bqhd,bkhd->bhqk', q, k).astype(jnp.float32) * (HEAD_DIM ** -0.5)
    bias = jnp.swapaxes(c_q, 1, 2)[:, :, :, None] - jnp.swapaxes(c_k, 1, 2)[:, :, None, :]
    return s + bias


def fox_prompt(q, k, v, logf):
    B, S = q.shape[0], q.shape[1]
    nb = S // Q_BLOCK
    c = jnp.cumsum(logf, axis=1)
    k_pos = jnp.arange(S)
    qb = jnp.swapaxes(q.reshape(B, nb, Q_BLOCK, N_HEADS, HEAD_DIM), 0, 1)
    cb = jnp.swapaxes(c.reshape(B, nb, Q_BLOCK, N_HEADS), 0, 1)
    pb = k_pos.reshape(nb, Q_BLOCK)

    def one_block(args):
        q_blk, c_blk, q_pos = args
        s = fox_scores(q_blk, k, c_blk, c)
        mask = k_pos[None, :] <= q_pos[:, None]
        p = jax.nn.softmax(jnp.where(mask, s, -jnp.inf), axis=-1)
        return jnp.einsum('bhqk,bkhd->bqhd', p.astype(v.dtype), v)

    out = lax.map(one_block, (qb, cb, pb))
    return jnp.swapaxes(out, 0, 1).reshape(B, S, N_HEADS, HEAD_DIM)


def fox_sample(q, k, v, logf, cache_k, cache_v, cache_logf, page_table):
    DB, T = q.shape[0], q.shape[1]
    past_k = cache_k[page_table].reshape(DB, -1, N_HEADS, HEAD_DIM)
    past_v = cache_v[page_table].reshape(DB, -1, N_HEADS, HEAD_DIM)
    past_lf = cache_logf[page_table].reshape(DB, -1, N_HEADS).astype(jnp.float32)
    c_past = jnp.cumsum(past_lf, axis=1)
    c_new = c_past[:, -1:] + jnp.cumsum(logf, axis=1)
    P = past_k.shape[1]
    s_past = fox_scores(q, past_k, c_new, c_past)
    s_new = fox_scores(q, k, c_new, c_new)
    pos = jnp.arange(T)
    s_new = jnp.where(pos[None, :] <= pos[:, None], s_new, -jnp.inf)
    p = jax.nn.softmax(jnp.concatenate([s_past, s_new], axis=-1), axis=-1)
    p_past, p_new = p[..., :P].astype(v.dtype), p[..., P:].astype(v.dtype)
    return (jnp.einsum('bhqk,bkhd->bqhd', p_past, past_v)
            + jnp.einsum('bhqk,bkhd->bqhd', p_new, v))


def rglru_branch(xr, yr, conv_buf, h0, conv_w, conv_b, w_rg, b_rg, w_ig, b_ig, lru_lambda):
    B, T = xr.shape[0], xr.shape[1]
    xc = jnp.concatenate([conv_buf.astype(xr.dtype), xr], axis=1)
    conv = conv_b
    for i in range(CONV_WIDTH):
        conv = conv + xc[:, i:i + T] * conv_w[i]
    new_buf = xc[:, T:]
    xb = conv.reshape(B, T, N_RNN_BLOCKS, RNN_BLOCK)
    r = jax.nn.sigmoid(jnp.einsum('btnd,nde->btne', xb, w_rg).reshape(B, T, D_RNN) + b_rg)
    i_g = jax.nn.sigmoid(jnp.einsum('btnd,nde->btne', xb, w_ig).reshape(B, T, D_RNN) + b_ig)
    log_a = -LRU_C * r.astype(jnp.float32) * jax.nn.softplus(-lru_lambda.astype(jnp.float32))
    a = jnp.exp(log_a)
    b = jnp.sqrt(-jnp.expm1(2.0 * log_a)) * (i_g * conv).astype(jnp.float32)
    b = b.at[:, 0].add(a[:, 0] * h0.astype(jnp.float32))

    def combine(left, right):
        a_l, b_l = left
        a_r, b_r = right
        return a_l * a_r, a_r * b_l + b_r

    _, h = lax.associative_scan(combine, (a, b), axis=1)
    out = h.astype(xr.dtype) * jax.nn.gelu(yr)
    return out, h[:, -1], new_buf


def trunk_layer(x, attn_fn, conv_buf, h0, g_pre_mix, w_in, b_f, conv_w, conv_b, w_rg, b_rg,
                w_ig, b_ig, lru_lambda, w_o_attn, w_o_rnn, w_out, g_post_mix, g_pre_ffn,
                w_gate, w_up, w_down, g_post_ffn):
    B, T, _ = x.shape
    xn = rms_norm(x, g_pre_mix)
    q, k, v, f, xr, yr, ga, gr = split_columns(xn @ w_in)
    q = q.reshape(B, T, N_HEADS, HEAD_DIM)
    k = k.reshape(B, T, N_HEADS, HEAD_DIM)
    v = v.reshape(B, T, N_HEADS, HEAD_DIM)
    logf = jax.nn.log_sigmoid((f + b_f).astype(jnp.float32))
    attn = attn_fn(q, k, v, logf)
    rnn, h_last, new_buf = rglru_branch(xr, yr, conv_buf, h0, conv_w, conv_b, w_rg, b_rg,
                                        w_ig, b_ig, lru_lambda)
    o = (jax.nn.sigmoid(ga) * (attn.reshape(B, T, ATTN_WIDTH) @ w_o_attn)
         + jax.nn.sigmoid(gr) * (rnn @ w_o_rnn))
    h = x + rms_norm(o @ w_out, g_post_mix)
    hn = rms_norm(h, g_pre_ffn)
    ff = (jax.nn.silu(hn @ w_gate) * (hn @ w_up)) @ w_down
    y = h + rms_norm(ff, g_post_ffn)
    return y, k, v, logf, h_last, new_buf


def setup_inputs(seed: int = 0) -> dict:
    key = jax.random.key(seed)
    ks = iter(jax.random.split(key, 40))
    nrm = lambda shape, scale: jax.random.normal(next(ks), shape, jnp.float32) * scale
    n_pages = PAST_LEN // PAGE_SIZE
    n_used = DEC_BATCH * n_pages
    n_phys = (5 * n_used + 3) // 4
    page_table = jax.random.permutation(next(ks), n_phys)[:n_used].reshape(DEC_BATCH, n_pages).astype(jnp.int32)
    a0 = jax.random.uniform(next(ks), (DEPTH, D_RNN), jnp.float32, 0.9, 0.999)
    a_base = a0 ** (1.0 / LRU_C)
    lru_lambda = jnp.log(a_base) - jnp.log1p(-a_base)
    return {
        'x_prompt': nrm((BATCH, SEQ, D_MODEL), 1.0),
        'x_sample': nrm((DEC_BATCH, DEC_SEQ, D_MODEL), 1.0),
        'cache_k': nrm((DEPTH, n_phys, PAGE_SIZE, N_HEADS, HEAD_DIM), 1.0),
        'cache_v': nrm((DEPTH, n_phys, PAGE_SIZE, N_HEADS, HEAD_DIM), 1.0),
        'cache_logf': jax.nn.log_sigmoid(2.0 + nrm((DEPTH, n_phys, PAGE_SIZE, N_HEADS), 0.5)),
        'state_h': nrm((DEPTH, DEC_BATCH, D_RNN), 0.5),
        'state_conv': nrm((DEPTH, DEC_BATCH, CONV_WIDTH - 1, D_RNN), 1.0),
        'page_table': page_table,
        'g_pre_mix': 1.0 + nrm((DEPTH, D_MODEL), 0.05),
        'w_in': nrm((DEPTH, D_MODEL, IN_COLS), D_MODEL ** -0.5),
        'b_f': 2.0 + nrm((DEPTH, N_HEADS), 0.1),
        'conv_w': nrm((DEPTH, CONV_WIDTH, D_RNN), CONV_WIDTH ** -0.5),
        'conv_b': nrm((DEPTH, D_RNN), 0.01),
        'w_rg': nrm((DEPTH, N_RNN_BLOCKS, RNN_BLOCK, RNN_BLOCK), RNN_BLOCK ** -0.5),
        'b_rg': nrm((DEPTH, D_RNN), 0.01),
        'w_ig': nrm((DEPTH, N_RNN_BLOCKS, RNN_BLOCK, RNN_BLOCK), RNN_BLOCK ** -0.5),
        'b_ig': nrm((DEPTH, D_RNN), 0.01),
        'lru_lambda': lru_lambda,
        'w_o_attn': nrm((DEPTH, ATTN_WIDTH, D_MODEL), ATTN_WIDTH ** -0.5),
        'w_o_rnn': nrm((DEPTH, D_RNN, D_MODEL), D_RNN ** -0.5),
        'w_out': nrm((DEPTH, D_MODEL, D_MODEL), D_MODEL ** -0.5),
        'g_post_mix': 1.0 + nrm((DEPTH, D_MODEL), 0.05),
        'g_pre_ffn': 1.0 + nrm((DEPTH, D_MODEL), 0.05),
        'w_gate': nrm((DEPTH, D_MODEL, D_FF), D_MODEL ** -0.5),
        'w_up': nrm((DEPTH, D_MODEL, D_FF), D_MODEL ** -0.5),
        'w_down': nrm((DEPTH, D_FF, D_MODEL), D_FF ** -0.5),
        'g_post_ffn': 1.0 + nrm((DEPTH, D_MODEL), 0.05),
    }


def reference(x_prompt, x_sample, cache_k, cache_v, cache_logf, state_h, state_conv, page_table,
              g_pre_mix, w_in, b_f, conv_w, conv_b, w_rg, b_rg, w_ig, b_ig, lru_lambda,
              w_o_attn, w_o_rnn, w_out, g_post_mix, g_pre_ffn, w_gate, w_up, w_down, g_post_ffn):
    xp, xs = x_prompt, x_sample
    kp, vp, lp, hp, cp = [], [], [], [], []
    ksm, vsm, lsm, hsm, csm = [], [], [], [], []
    for l in range(DEPTH):
        w = (g_pre_mix[l], w_in[l], b_f[l], conv_w[l], conv_b[l], w_rg[l], b_rg[l], w_ig[l],
             b_ig[l], lru_lambda[l], w_o_attn[l], w_o_rnn[l], w_out[l], g_post_mix[l],
             g_pre_ffn[l], w_gate[l], w_up[l], w_down[l], g_post_ffn[l])
        zero_buf = jnp.zeros((xp.shape[0], CONV_WIDTH - 1, D_RNN), xp.dtype)
        zero_h = jnp.zeros((xp.shape[0], D_RNN), jnp.float32)
        xp, k1, v1, lf1, h1, c1 = trunk_layer(xp, fox_prompt, zero_buf, zero_h, *w)
        ck, cv, cl = cache_k[l], cache_v[l], cache_logf[l]
        attn_fn = lambda q, k, v, lf, ck=ck, cv=cv, cl=cl: fox_sample(q, k, v, lf, ck, cv, cl, page_table)
        xs, k2, v2, lf2, h2, c2 = trunk_layer(xs, attn_fn, state_conv[l], state_h[l], *w)
        kp.append(k1); vp.append(v1); lp.append(lf1); hp.append(h1); cp.append(c1)
        ksm.append(k2); vsm.append(v2); lsm.append(lf2); hsm.append(h2); csm.append(c2)
    k_prompt, v_prompt, logf_prompt = jnp.stack(kp), jnp.stack(vp), jnp.stack(lp)
    h_prompt, conv_prompt = jnp.stack(hp), jnp.stack(cp)
    k_sample, v_sample, logf_sample = jnp.stack(ksm), jnp.stack(vsm), jnp.stack(lsm)
    h_sample, conv_sample = jnp.stack(hsm), jnp.stack(csm)
    return (xp, xs, k_prompt, v_prompt, logf_prompt, h_prompt, conv_prompt,
            k_sample, v_sample, logf_sample, h_sample, conv_sample)
```

```python
import numpy as np
import concourse.bass as bass
import concourse.mybir as mybir
from concourse.bass_utils import run_bass_kernel_spmd

F32 = mybir.dt.float32
BF16 = mybir.dt.bfloat16
I32 = mybir.dt.int32
ALU = mybir.AluOpType
AF = mybir.ActivationFunctionType
AX = mybir.AxisListType

P = 128
D = 2048
KC = 16
T = 1024
NT = 8
NS = 16
TT = T + NS
H = 8
HD = 128
R = 1024
RC = 8
F = 5632
FC = 44
PAGES = 16
OFF_Q, OFF_K, OFF_V, OFF_F, OFF_XR, OFF_YR, OFF_GA, OFF_GR = 0, 1024, 2048, 3072, 3080, 4104, 5128, 7176
IN_COLS = 9224
EPS = 1e-6
SCALE = float(HD) ** -0.5
SQD = float(HD) ** 0.5
NEG = -30000.0
N_CORES = 8

ENGS = ("pe", "act", "dve", "pool", "sp")


class Prog:
    def __init__(self, nc):
        self.nc = nc
        self.ops = {e: [] for e in ENGS}
        self.count = {e: 0 for e in ENGS}
        self.dma_count = {}
        self.grp_count = {}
        self.last_w = {}
        self.readers = {}
        self.waited = {e: {} for e in ENGS}
        self.enabled = True
        self.stop_after = None

    def cut(self, name):
        if self.stop_after == name:
            self.enabled = False

    def _deps(self, eng, rd, wr, is_dma):
        deps = []
        own = "E_" + eng
        for r in rd:
            t = self.last_w.get(r)
            if t is not None:
                deps.append(t)
            if isinstance(r, tuple) and r[0] == "ps":
                for t in self.readers.get(r, ()):
                    if t[0] != own:
                        deps.append(t)
        skip_own = (eng == "pe") and not is_dma
        for w in wr:
            t = self.last_w.get(w)
            if t is not None and not (skip_own and t[0] == own):
                deps.append(t)
            for t in self.readers.get(w, ()):
                if not (skip_own and t[0] == own):
                    deps.append(t)
        best = {}
        for s, v in deps:
            if v > best.get(s, 0):
                best[s] = v
        waits = []
        wd = self.waited[eng]
        for s, v in best.items():
            if v > wd.get(s, 0):
                wd[s] = v
                waits.append((s, v))
        return waits

    def _commit(self, tok, rd, wr):
        for r in rd:
            self.readers.setdefault(r, []).append(tok)
        for w in wr:
            self.last_w[w] = tok
            self.readers[w] = []

    def op(self, eng, meth, rd=(), wr=(), **kw):
        if not self.enabled:
            return
        waits = self._deps(eng, rd, wr, False)
        self.count[eng] += 1
        tok = ("E_" + eng, self.count[eng])
        self._commit(tok, rd, wr)
        self.ops[eng].append((waits, (meth, kw), ("E_" + eng, 1)))

    def dma(self, queue, sem, rd=(), wr=(), meth="dma_start", K=1, **kw):
        if not self.enabled:
            return
        n = self.grp_count.get(sem, 0)
        self.grp_count[sem] = n + 1
        phys = "%s_r%d" % (sem, n % K)
        waits = self._deps(queue, rd, wr, True)
        c = self.dma_count.get(phys, 0)
        if c > 0 and 16 * c > self.waited[queue].get(phys, 0):
            self.waited[queue][phys] = 16 * c
            waits.append((phys, 16 * c))
        self.dma_count[phys] = c + 1
        tok = (phys, 16 * (c + 1))
        self._commit(tok, rd, wr)
        self.ops[queue].append((waits, (meth, kw), (phys, 16)))

    def barrier(self, engines=ENGS):
        if not self.enabled:
            return
        toks = [("E_" + e, self.count[e]) for e in ENGS if self.count[e] > 0]
        toks += [(s, 16 * c) for s, c in self.dma_count.items()]
        for e in engines:
            wd = self.waited[e]
            waits = []
            for s, v in toks:
                if s == "E_" + e:
                    continue
                if v > wd.get(s, 0):
                    wd[s] = v
                    waits.append((s, v))
            if waits:
                self.ops[e].append((waits, None, None))

    def final_wait(self):
        wd = self.waited["sp"]
        waits = []
        for s, c in self.dma_count.items():
            if 16 * c > wd.get(s, 0):
                waits.append((s, 16 * c))
        for e in ENGS:
            if e != "sp" and self.count[e] > 0:
                waits.append(("E_" + e, self.count[e]))
        self.ops["sp"].append((waits, None, None))

    def sem_names(self):
        names = ["E_" + e for e in ENGS if self.count[e] > 0]
        names += list(self.dma_count.keys())
        return names

    def emit(self, block, sems):
        def runner(eng_name):
            def run(e):
                for waits, fn, inc in self.ops[eng_name]:
                    for s, v in waits:
                        e.wait_ge(sems[s], v)
                    if fn is not None:
                        ins = getattr(e, fn[0])(**fn[1])
                        ins.then_inc(sems[inc[0]], inc[1])
            return run
        block.tensor(runner("pe"))
        block.scalar(runner("act"))
        block.vector(runner("dve"))
        block.gpsimd(runner("pool"))
        block.sync(runner("sp"))


def build_program(n_phys, with_samples=True, with_sattn=True, stop_after=None, dbg=0):
    nc = bass.Bass("TRN2", target_bir_lowering=False)
    pg = Prog(nc)
    pg.stop_after = stop_after

    def din(name, shape, dt=F32):
        return nc.dram_tensor(name, list(shape), dt, kind="ExternalInput").ap()

    def dout(name, shape, dt=F32):
        return nc.dram_tensor(name, list(shape), dt, kind="ExternalOutput").ap()

    x_own = din("x_own", [T, D])
    x_pre = din("x_pre", [T, D])
    x_smp = din("x_smp", [NS, D])
    flagd = din("flag", [P, 2])
    cst = din("cst", [P, 6 * 128])
    sel3d = din("sel3", [24, H * 128])
    segd = din("segm", [P, 2 * H * 17])
    bmaskd = din("bmask", [H, H * HD])
    iotad = din("iota", [P, 1])
    cache_k = din("cache_k", [n_phys * 128, H * HD])
    cache_v = din("cache_v", [n_phys * 128, H * HD])
    cache_lf = din("cache_lf", [n_phys, 128 * H])
    state_h = din("state_h", [NS, R])
    state_conv = din("state_conv", [NS, 3 * R])
    ptab = din("ptab", [1, NS * PAGES], I32)
    ptab16 = din("ptab16", [PAGES, NS], I32)
    g_pre_mix = din("g_pre_mix", [KC, 128])
    g_pre_ffn = din("g_pre_ffn", [KC, 128])
    g_post_mix = din("g_post_mix", [1, D])
    g_post_ffn = din("g_post_ffn", [1, D])
    w_in = din("w_in", [D, IN_COLS])
    b_f = din("b_f", [1, H])
    rnnp = din("rnnp", [64, 128])
    w_rg = din("w_rg", [RC, 128, 128])
    w_ig = din("w_ig", [RC, 128, 128])
    w_o_attn = din("w_o_attn", [H * HD, D])
    w_o_rnn = din("w_o_rnn", [R, D])
    w_out = din("w_out", [D, D])
    w_gate = din("w_gate", [D, F])
    w_up = din("w_up", [D, F])
    w_down = din("w_down", [F, D])

    y_own = dout("y_own", [T, D])
    y_smp = dout("y_smp", [NS, D])
    k_own = dout("k_own", [T, H * HD])
    v_own = dout("v_own", [T, H * HD])
    lf_own = dout("lf_own", [T, H])
    h_own = dout("h_own", [RC, 128])
    conv_own = dout("conv_own", [3 * RC, 128])
    k_smp = dout("k_smp", [NS, H * HD])
    v_smp = dout("v_smp", [NS, H * HD])
    lf_smp = dout("lf_smp", [NS, H])
    h_smp = dout("h_smp", [NS, R])
    conv_smp = dout("conv_smp", [NS, 3 * R])
    scr_q = nc.dram_tensor("scr_q", [NS, H * HD], F32, kind="Internal").ap()
    scr_nlf = nc.dram_tensor("scr_nlf", [NS, H], F32, kind="Internal").ap()

    from contextlib import ExitStack
    es = ExitStack()

    def sb(name, shape, dt=F32):
        return es.enter_context(nc.sbuf_tensor(name, list(shape), dt))

    A_BYTES = 178048
    arena = sb("arena", [P, A_BYTES // 4], F32)

    def aview(off, nbytes, dt):
        v = arena[:, off // 4:(off + nbytes) // 4]
        return v if dt == F32 else v.bitcast(dt)

    O_G1, O_G2, O_G3, O_G4, O_G5, O_RT, O_XT, O_MISC = 0, 33280, 66560, 99328, 115968, 132608, 148992, 165376
    xnT = aview(O_G1, 33280, BF16).rearrange("p (c t) -> p c t", c=KC)
    KT = aview(O_G2, 32768, BF16).rearrange("p (h t) -> p h t", h=H)
    oT = aview(O_G2, 33280, BF16).rearrange("p (c t) -> p c t", c=KC)
    Vt = aview(O_G3, 32768, BF16).rearrange("p (t c) -> p t c", t=2 * NT)
    QA = aview(O_G4, 16640, BF16).rearrange("p (h t) -> p h t", h=H)
    rnnT = aview(O_G5, 16640, BF16).rearrange("p (h t) -> p h t", h=RC)
    XT = [aview(O_XT + i * 8192, 8192, F32) for i in range(2)]
    WOUT = aview(O_G3, 65536, BF16).rearrange("p (k c) -> p k c", k=KC)
    GBC = aview(O_RT, 8192, F32)
    actT = aview(O_G2, 91520, BF16).rearrange("p (f t) -> p f t", f=FC)
    FFt = [aview(i * 8192, 8192, F32) for i in range(4)] + [aview(124800 + i * 8192, 8192, F32) for i in range(5)]
    XR = aview(O_RT, 4128, F32)
    CV = aview(O_RT + 4128, 2048, F32)
    CVB = aview(O_RT + 6176, 1024, BF16)
    TA = aview(O_RT + 7200, 2048, F32)
    TB = aview(O_RT + 9248, 2048, F32)
    TC = aview(O_RT + 11296, 2048, F32)
    HS = aview(O_RT + 13344, 2048, F32)
    mo = [O_MISC]

    def misc(nbytes, dt):
        v = aview(mo[0], nbytes, dt)
        mo[0] += nbytes
        assert mo[0] <= A_BYTES
        return v
    LF = misc(512, F32).rearrange("p (t h) -> p t h", t=2 * NT)
    CTk = misc(512, F32).rearrange("p (t h) -> p t h", t=2 * NT)
    BKb = misc(512, F32).rearrange("p (t h) -> p t h", t=2 * NT)
    ZT = misc(512, F32)
    C3 = misc(2048, BF16)
    O_OST = mo[0]
    OST = [misc(1024, F32) for _ in range(2)]
    O_SST = mo[0]
    SST = [misc(1024, F32) for _ in range(2)]
    O_JNK = mo[0]
    JNK = misc(4096, BF16)
    LFS = misc(64, F32)

    WB = [sb("wb%d" % i, [P, 4096], BF16) for i in range(2)]
    cident = sb("cident", [P, 128], F32)
    cutri = sb("cutri", [P, 128], F32)
    clstr = sb("clstr", [P, 128], F32)
    conesf = sb("conesf", [P, 128], F32)
    identb = sb("identb", [P, 128], BF16)
    ubias = sb("ubias", [P, 128], BF16)
    onesb = sb("onesb", [P, 128], BF16)
    sel3 = sb("sel3s", [P, H * 128], BF16)
    flag = sb("flags", [P, 2], F32)
    rpraw = sb("rpraw", [P, 128], F32)
    rp = sb("rp", [P, 96], F32)
    nsp = sb("nsp", [P, RC], F32)
    wrg = sb("wrg", [P, RC * 128], BF16)
    wig = sb("wig", [P, RC * 128], BF16)
    bfb = sb("bfb", [P, H], F32)
    hmid = sb("hmid", [P, RC], F32)
    hist = sb("hist", [P, RC * 3], F32)
    stat = sb("stat", [P, 16], F32)
    hl = sb("hl", [P, RC], F32)
    cvo = sb("cvo", [P, RC * 3], F32)
    XRS = sb("xrs", [P, RC, 4 * NS], F32)
    YRS = sb("yrs", [P, RC, NS], F32)
    H0S = sb("h0s", [P, RC * NS], F32)
    HNS = sb("hns", [P, RC * NS], F32)
    stat8 = sb("stat8", [P, 16], F32)
    IDX = sb("idx", [P, NS * PAGES], I32)
    PT16 = sb("pt16", [P, NS], I32)
    iot = sb("iot_s", [P, 1], F32)
    segm = sb("segm_s", [P, 2 * H * 17], F32)
    NLF1 = sb("nlf1", [P, H], F32)
    GS = sb("gs_s", [P, KC * NS], F32)
    RRS = sb("rr_s", [P, KC * NS], F32)
    GBC7 = WB[0][:, :].bitcast(F32)
    HST7 = WB[1][:, :].bitcast(F32)

    ps = [es.enter_context(nc.psum_tensor("ps%d" % i, [P, 512], F32)) for i in range(8)]

    def psb(i):
        return ps[i][:, :].bitcast(BF16)

    def ld(dst, src, q="sp", sem="s_c", wr=()):
        pg.dma(q, sem, rd=(), wr=wr, K=4, out=dst, in_=src)

    ld(cident[:], cst[:, 0:128], wr=("cident",))
    ld(cutri[:], cst[:, 128:256], wr=("cutri",))
    ld(clstr[:], cst[:, 256:384], wr=("clstr",))
    ld(conesf[:], cst[:, 512:640], wr=("conesf",))
    ld(identb[:], cst[:, 0:128], q="pool", sem="s_cp", wr=("identb",))
    ld(ubias[:], cst[:, 384:512], q="pool", sem="s_cp", wr=("ubias",))
    ld(onesb[:], cst[:, 512:640], q="pool", sem="s_cp", wr=("onesb",))
    ld(sel3[0:24, :], sel3d[:, :], q="pool", sem="s_cp", wr=("sel3",))
    ld(flag[:], flagd[:, :], wr=("flag",))
    ld(rpraw[0:64, :], rnnp[:, :], wr=("rpraw",))
    ld(rpraw[64:80, :], g_pre_mix[:, :], wr=("rpraw",))
    ld(rpraw[80:96, :], g_pre_ffn[:, :], wr=("rpraw",))
    ld(bfb[:], b_f[0:1, :].partition_broadcast(P), wr=("bfb",))
    ld(wrg[:].rearrange("p (n e) -> p n e", n=RC), w_rg.rearrange("n d e -> d n e"), q="pool", sem="s_cp", wr=("wrg",))
    ld(wig[:].rearrange("p (n e) -> p n e", n=RC), w_ig.rearrange("n d e -> d n e"), q="pool", sem="s_cp", wr=("wig",))
    pg.op("pe", "transpose", rd=("rpraw", "cident"), wr=(("ps", 7),), out=ps[7][:, 0:96], in_=rpraw[0:96, :], identity=cident[0:96, 0:96])
    pg.op("dve", "tensor_copy", rd=(("ps", 7),), wr=("rp",), out=rp[:], in_=ps[7][:, 0:96])
    pg.op("act", "activation", rd=("rp",), wr=("nsp",), out=nsp[:], in_=rp[:, 56:64], func=AF.Exp, scale=-1.0)
    pg.op("act", "activation", rd=("nsp",), wr=("nsp",), out=nsp[:], in_=nsp[:], func=AF.Ln, bias=1.0)
    pg.op("dve", "tensor_scalar", rd=("nsp",), wr=("nsp",), out=nsp[:], in0=nsp[:], scalar1=-8.0, scalar2=None, op0=ALU.mult)

    pg.cut("P0")
    slab_list = []

    def add_slab(parts):
        slab_list.append(parts)
        return len(slab_list) - 1

    def wslab(src2d, col0, ncols):
        return src2d[:, col0:col0 + ncols].rearrange("(k p) c -> p k c", p=P)

    slab_state = {"next": 0}

    def slab_view(slot, off, kn, ncols):
        return WB[slot][:, off:off + kn * ncols].rearrange("p (k c) -> p k c", k=kn)

    def slab_issue(i):
        if i >= len(slab_list) or i < slab_state["next"]:
            return
        assert i == slab_state["next"]
        slot = i % 2
        for pi_, (off, kn, ncols, src) in enumerate(slab_list[i]):
            pg.dma("pool", "s_w%d_%d" % (slot, pi_), wr=(("wb", slot),), out=slab_view(slot, off, kn, ncols), in_=src)
        slab_state["next"] = i + 1

    def slab_get(i):
        slab_issue(i)
        slab_issue(i + 1)
        slot = i % 2
        return slot, [slab_view(slot, off, kn, ncols) for (off, kn, ncols, src) in slab_list[i]]

    SL = {}
    SL["p1k"] = [add_slab([(0, KC, 256, wslab(w_in, OFF_K + 256 * s, 256))]) for s in range(4)]
    SL["p1v"] = [add_slab([(0, KC, 256, wslab(w_in, OFF_V + 256 * s, 256))]) for s in range(4)]
    SL["p1f"] = add_slab([(0, KC, 8, wslab(w_in, OFF_F, 8))])
    SL["p1x"] = [add_slab([(0, KC, 256, wslab(w_in, OFF_XR + 256 * s, 256))]) for s in range(4)]
    SL["p2q"] = [add_slab([(0, KC, 256, wslab(w_in, OFF_Q + 256 * s, 256))]) for s in range(4)]
    SL["p2k"] = [add_slab([(0, KC, 256, wslab(w_in, OFF_K + 256 * s, 256))]) for s in range(4)]
    SL["p2v"] = [add_slab([(0, KC, 256, wslab(w_in, OFF_V + 256 * s, 256))]) for s in range(4)]
    SL["p2f"] = add_slab([(0, KC, 8, wslab(w_in, OFF_F, 8))])
    SL["p2x"] = [add_slab([(0, KC, 128, wslab(w_in, OFF_XR + 128 * n, 128)), (2048, KC, 128, wslab(w_in, OFF_YR + 128 * n, 128))])
                 for n in range(RC)]
    SL["p4"] = []
    for c in range(KC):
        a = add_slab([(0, KC, 128, wslab(w_in, OFF_GA + 128 * c, 128)), (2048, 8, 128, wslab(w_o_attn, 128 * c, 128))])
        b = add_slab([(0, KC, 128, wslab(w_in, OFF_GR + 128 * c, 128)), (2048, 8, 128, wslab(w_o_rnn, 128 * c, 128))])
        SL["p4"].append((a, b))
    SL["p4b"] = [add_slab([(0, 8, 512, wslab(w_o_attn, 512 * s4, 512))]) for s4 in range(4)]
    SL["p6"] = [add_slab([(0, KC, 128, wslab(w_gate, 128 * fc, 128)), (2048, KC, 128, wslab(w_up, 128 * fc, 128))])
                for fc in range(FC)]
    SL["p7"] = []
    for c in range(KC):
        a = add_slab([(0, 22, 128, w_down[0:22 * 128, 128 * c:128 * c + 128].rearrange("(k p) c -> p k c", p=P))])
        b = add_slab([(0, 22, 128, w_down[22 * 128:44 * 128, 128 * c:128 * c + 128].rearrange("(k p) c -> p k c", p=P))])
        SL["p7"].append((a, b))

    psrot = {"i": 0}

    def psum_next(banks):
        b = banks[psrot["i"] % len(banks)]
        psrot["i"] += 1
        return b

    evt = {"i": 0}

    def alt():
        evt["i"] += 1
        return "act" if evt["i"] % 2 else "dve"

    def evac(eng, out, in_, rd, wr):
        if eng == "act":
            pg.op("act", "activation", rd=rd, wr=wr, out=out, in_=in_, func=AF.Copy)
        else:
            pg.op("dve", "tensor_copy", rd=rd, wr=wr, out=out, in_=in_)

    def rstd_from(src, m, slot, src_res, nparts=1):
        sr = ("stat", slot)
        c0 = slot * 4
        if nparts == 1:
            pg.op("act", "activation", rd=src_res, wr=("JNK", sr), out=JNK[0:m, :], in_=src[0], func=AF.Square,
                  accum_out=stat[0:m, c0:c0 + 1])
        else:
            for i in range(nparts):
                pg.op("act", "activation", rd=src_res, wr=("JNK", ("st8", slot)), out=JNK[0:m, 0:512], in_=src[i], func=AF.Square,
                      accum_out=stat8[0:m, slot * 4 + i:slot * 4 + i + 1])
            pg.op("dve", "tensor_reduce", rd=(("st8", slot),), wr=(sr,), out=stat[0:m, c0:c0 + 1], in_=stat8[0:m, slot * 4:slot * 4 + 4],
                  axis=AX.X, op=ALU.add)
        pg.op("act", "activation", rd=(sr,), wr=(sr,), out=stat[0:m, c0 + 1:c0 + 2], in_=stat[0:m, c0:c0 + 1], func=AF.Sqrt,
              scale=1.0 / D, bias=EPS)
        pg.op("dve", "reciprocal", rd=(sr,), wr=(sr,), out=stat[0:m, c0 + 2:c0 + 3], in_=stat[0:m, c0 + 1:c0 + 2])
        return stat[0:m, c0 + 2:c0 + 3], sr

    def norm_transpose(xt, xres, m, slot, dst, dst_cols, gcol0, dst_res):
        rs, sr = rstd_from([xt], m, slot, (xres,))
        pg.op("dve", "tensor_scalar", rd=(xres, sr, "JNK"), wr=("JNK",), out=JNK[0:m, :], in0=xt, scalar1=rs, scalar2=None, op0=ALU.mult)
        for half in range(2):
            bank = 6 + half
            pv = psb(bank)
            for j in range(8):
                kc = half * 8 + j
                pg.op("pe", "transpose", rd=("JNK", "identb"), wr=(("ps", bank),), out=pv[:, j * 128:j * 128 + m],
                      in_=JNK[0:m, kc * 128:(kc + 1) * 128], identity=identb[0:m, 0:m])
            src = pv.rearrange("p (c t) -> p c t", c=8)[:, :, 0:m]
            g = rp[:, gcol0 + half * 8:gcol0 + half * 8 + 8].unsqueeze(2).to_broadcast([P, 8, m])
            pg.op("dve", "tensor_tensor", rd=(("ps", bank), "rp"), wr=(dst_res,), out=dst[:, half * 8:half * 8 + 8, dst_cols:dst_cols + m],
                  in0=src, in1=g, op=ALU.mult)

    def load_norm_tile(src_rows, m, slot, dst, dst_cols, gcol0, dst_res):
        xres = ("XT", slot)
        pg.dma("sp", "s_x%d" % slot, wr=(xres,), out=XT[slot][0:m, :], in_=src_rows)
        norm_transpose(XT[slot][0:m, :], xres, m, slot, dst, dst_cols, gcol0, dst_res)

    def mm_fm(bank, wv, cc, act, kn, c0, n, act_res, slot, first=True, last=True, kbase=0):
        for kc in range(kn):
            pg.op("pe", "matmul", rd=(("wb", slot),) + act_res, wr=(("ps", bank),), out=ps[bank][:, 0:n],
                  lhsT=wv[:, kc, cc * 128:(cc + 1) * 128], rhs=act[:, kbase + kc, c0:c0 + n],
                  start=(first and kc == 0), stop=(last and kc == kn - 1))

    def mm_tm(bank, wv, act, kn, t0, m, ncols, act_res, slot, col0=0):
        for kc in range(kn):
            pg.op("pe", "matmul", rd=(("wb", slot),) + act_res, wr=(("ps", bank),), out=ps[bank][0:m, 0:ncols],
                  lhsT=act[:, kc, t0:t0 + m], rhs=wv[:, kc, col0:col0 + ncols], start=(kc == 0), stop=(kc == kn - 1))

    def logf_from_psum(bank, m, ntile, out_view, out_res):
        n = ntile * 8
        z = ZT[0:m, 0:n]
        pg.op("dve", "tensor_tensor", rd=(("ps", bank), "bfb"), wr=("ZT",), out=z.rearrange("p (t h) -> p t h", h=H),
              in0=ps[bank][0:m, 0:n].rearrange("p (t h) -> p t h", h=H),
              in1=bfb[0:m, :].unsqueeze(1).to_broadcast([m, ntile, H]), op=ALU.add)
        pg.op("act", "activation", rd=("ZT",), wr=("ZT",), out=z, in_=z, func=AF.Exp, scale=-1.0)
        pg.op("act", "activation", rd=("ZT",), wr=("ZT",), out=z, in_=z, func=AF.Ln, bias=1.0)
        pg.op("dve", "tensor_scalar", rd=("ZT",), wr=(out_res,), out=out_view, in0=z, scalar1=-1.0, scalar2=None, op0=ALU.mult)

    def rnn_core(n, w, xin, init, init_res, hs_out, last_col, gbanks=(4, 5)):
        cw = lambda i: rp[:, i * 8 + n:i * 8 + n + 1]
        xrd = ("XRh", ("XR", 0), ("XR", 1), "rp")
        pg.op("dve", "tensor_scalar", rd=xrd, wr=("CV",), out=CV[:, 0:w], in0=xin[:, 0:w], scalar1=cw(0),
              scalar2=rp[:, 32 + n:33 + n], op0=ALU.mult, op1=ALU.add)
        for i in range(1, 4):
            pg.op("dve", "scalar_tensor_tensor", rd=xrd + ("CV",), wr=("CV",), out=CV[:, 0:w], in0=xin[:, i:i + w],
                  scalar=cw(i), in1=CV[:, 0:w], op0=ALU.mult, op1=ALU.add)
        pg.op("act", "activation", rd=("CV",), wr=("CVB",), out=CVB[:, 0:w], in_=CV[:, 0:w], func=AF.Copy)
        g0, g1 = gbanks
        pg.op("pe", "matmul", rd=("CVB", "wrg"), wr=(("ps", g0),), out=ps[g0][:, 0:w], lhsT=wrg[:, n * 128:(n + 1) * 128],
              rhs=CVB[:, 0:w], start=True, stop=True)
        pg.op("act", "activation", rd=(("ps", g0), "rp"), wr=("TA",), out=TA[:, 0:w], in_=ps[g0][:, 0:w], func=AF.Sigmoid,
              bias=rp[:, 40 + n:41 + n])
        pg.op("pe", "matmul", rd=("CVB", "wig"), wr=(("ps", g1),), out=ps[g1][:, 0:w], lhsT=wig[:, n * 128:(n + 1) * 128],
              rhs=CVB[:, 0:w], start=True, stop=True)
        pg.op("act", "activation", rd=(("ps", g1), "rp"), wr=("TB",), out=TB[:, 0:w], in_=ps[g1][:, 0:w], func=AF.Sigmoid,
              bias=rp[:, 48 + n:49 + n])
        pg.op("act", "activation", rd=("TA", "nsp"), wr=("TA",), out=TA[:, 0:w], in_=TA[:, 0:w], func=AF.Exp, scale=nsp[:, n:n + 1])
        pg.op("dve", "tensor_tensor", rd=("TA",), wr=("TC",), out=TC[:, 0:w], in0=TA[:, 0:w], in1=TA[:, 0:w], op=ALU.mult)
        pg.op("act", "activation", rd=("TC",), wr=("TC",), out=TC[:, 0:w], in_=TC[:, 0:w], func=AF.Sqrt, scale=-1.0, bias=1.0)
        pg.op("dve", "tensor_tensor", rd=("TB", "CV"), wr=("TB",), out=TB[:, 0:w], in0=TB[:, 0:w], in1=CV[:, 0:w], op=ALU.mult)
        pg.op("dve", "tensor_tensor", rd=("TB", "TC"), wr=("TB",), out=TB[:, 0:w], in0=TB[:, 0:w], in1=TC[:, 0:w], op=ALU.mult)
        pg.op("dve", "tensor_tensor_scan", rd=("TA", "TB") + init_res, wr=("HS",), out=HS[:, 0:w], data0=TA[:, 0:w], data1=TB[:, 0:w],
              initial=init, op0=ALU.mult, op1=ALU.add)

    def gelu_mul(n, w, ysrc, yres, out, out_res):
        pg.op("act", "activation", rd=yres, wr=("TC",), out=TC[:, 0:w], in_=ysrc, func=AF.Square)
        pg.op("dve", "tensor_scalar", rd=("TC",), wr=("TC",), out=TC[:, 0:w], in0=TC[:, 0:w], scalar1=0.044715, scalar2=1.0,
              op0=ALU.mult, op1=ALU.add)
        pg.op("dve", "tensor_tensor", rd=("TC",) + yres, wr=("TC",), out=TC[:, 0:w], in0=ysrc, in1=TC[:, 0:w], op=ALU.mult)
        pg.op("act", "activation", rd=("TC",), wr=("TC",), out=TC[:, 0:w], in_=TC[:, 0:w], func=AF.Sigmoid, scale=1.5957691216)
        pg.op("dve", "tensor_tensor", rd=("TC",) + yres, wr=("TC",), out=TC[:, 0:w], in0=ysrc, in1=TC[:, 0:w], op=ALU.mult)
        pg.op("dve", "tensor_tensor", rd=("TC", "HS"), wr=(out_res,), out=out, in0=TC[:, 0:w], in1=HS[:, 0:w], op=ALU.mult)

    def rnn_chunk(n, banks, own, yr_banks=None, gbanks=(4, 5)):
        if own:
            pg.op("dve", "tensor_copy", rd=("hist",), wr=("XRh",), out=XR[:, 0:3], in_=hist[:, n * 3:n * 3 + 3])
        else:
            pg.op("dve", "memset", wr=("XRh",), ap=XR[:, 0:3], constant=0.0)
        for g in range(2):
            evac("act", XR[:, 3 + g * 512:3 + (g + 1) * 512], ps[banks[g]][:, 0:512], rd=(("ps", banks[g]),), wr=(("XR", g),))
        for g in range(2):
            if g == 0:
                init, ires = (hmid[:, n:n + 1], ("hmid",)) if own else (0.0, ())
            else:
                init, ires = hl[:, n:n + 1], ("hl",)
            rnn_core(n, 512, XR[:, g * 512:g * 512 + 515], init, ires, HS, 511, gbanks=gbanks)
            pg.op("dve", "tensor_copy", rd=("HS",), wr=("hl",), out=hl[:, n:n + 1], in_=HS[:, 511:512])
            if own:
                yb = yr_banks[g]
                gelu_mul(n, 512, ps[yb][:, 0:512], (("ps", yb),), rnnT[:, n, g * 512:(g + 1) * 512], ("rnnT", n))
        if own:
            pg.op("dve", "tensor_copy", rd=(("XR", 1),), wr=("cvo",), out=cvo[:, n * 3:n * 3 + 3], in_=XR[:, 1024:1027])
        else:
            pg.op("dve", "tensor_scalar", rd=("hl", "flag"), wr=("hmid",), out=hmid[:, n:n + 1], in0=hl[:, n:n + 1],
                  scalar1=flag[:, 0:1], scalar2=None, op0=ALU.mult)
            pg.op("dve", "tensor_scalar", rd=(("XR", 1), "flag"), wr=("hist",), out=hist[:, n * 3:n * 3 + 3], in0=XR[:, 1024:1027],
                  scalar1=flag[:, 0:1], scalar2=None, op0=ALU.mult)

    stg = {"i": 0}

    def stage_out(bank, m, ncols, dram_ap, sem, stq, tag, wr=()):
        i = stg["i"] % 2
        stg["i"] += 1
        res = ("stg", tag, i)
        evac(alt(), stq[i][0:m, 0:ncols], ps[bank][0:m, 0:ncols], rd=(("ps", bank),), wr=(res,))
        pg.dma("sp", sem, rd=(res,), wr=wr, K=2, out=dram_ap, in_=stq[i][0:m, 0:ncols])

    B4 = [0, 1, 2, 3]
    GROUPS = [(0, 512), (512, 512), (T, NS)]

    NSLOT = 3
    KBs = [aview(O_XT + i * 2048, 2048, BF16) for i in range(NSLOT)]
    VBs = [aview(O_XT + 6144 + i * 2048, 2048, BF16) for i in range(NSLOT)]
    PROD = aview(O_XT + 12288, 2048, F32)
    QB = aview(O_XT + 14336, 2048, BF16)
    LFPv = aview(O_JNK, 4096, F32)
    O_SM = O_RT + 15392
    BPv = aview(O_SM, 544, F32).rearrange("p (h j) -> p h j", h=H)
    PBf = aview(O_SM + 544, 272, BF16).rearrange("p (h j) -> p h j", h=H)
    TJ = aview(O_SM + 816, 32, F32)
    ONs = aview(O_SM + 848, 32, F32)
    PRODb = aview(O_XT + 12288, 2048, BF16)
    LFsb = PROD[:, 0:128]
    CUM = PROD[:, 128:256].rearrange("p (h j) -> p h j", h=H)
    OSm = PROD

    def sample_gen():
        pg.dma("sp", "s_sm", wr=("IDX",), K=2, out=IDX[:, :], in_=ptab[0:1, :].partition_broadcast(P))
        pg.dma("sp", "s_sm", wr=("PT16",), K=2, out=PT16[0:PAGES, :], in_=ptab16[:, :])
        pg.dma("sp", "s_sm", wr=("iot",), K=2, out=iot[:, :], in_=iotad[:, :])
        pg.dma("sp", "s_sm", wr=("segm",), K=2, out=segm[:, :], in_=segd[:, :])
        idf = PROD
        pg.op("dve", "tensor_copy", rd=("IDX",), wr=("PROD", "PRODa", "PRODb"), out=idf[:, 0:NS * PAGES], in_=IDX[:, :])
        pg.op("dve", "tensor_scalar", rd=("PROD", "PRODa", "PRODb", "iot"), wr=("PROD", "PRODa", "PRODb"), out=idf[:, 0:NS * PAGES], in0=idf[:, 0:NS * PAGES], scalar1=128.0,
              scalar2=iot[:, 0:1], op0=ALU.mult, op1=ALU.add)
        pg.op("dve", "tensor_copy", rd=("PROD", "PRODa", "PRODb"), wr=("IDX",), out=IDX[:, :], in_=idf[:, 0:NS * PAGES])
        yield
        pcount = 0
        for b in range(NS):
            pg.dma("pool", "s_lfp", rd=("PT16",), wr=("JNK",), meth="indirect_dma_start", out=LFPv[0:PAGES, :], out_offset=None,
                   in_=cache_lf[:, :], in_offset=bass.IndirectOffsetOnAxis(ap=PT16[0:PAGES, b:b + 1], axis=0))
            pg.dma("pool", "s_qb", rd=("scr_q",), wr=("QB",), out=QB[:, :], in_=scr_q[b:b + 1, :].partition_broadcast(P))
            pg.dma("sp", "s_nl", rd=("scr_nlf",), wr=("NLF1",), out=NLF1[0:1, :], in_=scr_nlf[b:b + 1, :])
            lfp3 = LFPv[0:PAGES, :].rearrange("p (r h) -> p h r", h=H)
            for h in range(H):
                pg.op("pe", "transpose", rd=("JNK", "cident"), wr=(("ps", 6),), out=ps[6][:, h * PAGES:(h + 1) * PAGES], in_=lfp3[:, h, :],
                      identity=cident[0:PAGES, 0:PAGES])
            pg.op("act", "activation", rd=(("ps", 6),), wr=("PROD", "PRODa", "PRODb"), out=LFsb, in_=ps[6][:, 0:128], func=AF.Copy)
            pg.op("pe", "matmul", rd=("PROD", "PRODa", "PRODb", "clstr"), wr=(("ps", 6),), out=ps[6][:, 128:256], lhsT=clstr[:, :], rhs=LFsb, start=True, stop=True)
            pg.op("pe", "matmul", rd=("PROD", "PRODa", "PRODb", "conesf"), wr=(("ps", 6),), out=ps[6][:, 256:384], lhsT=conesf[:, :], rhs=LFsb, start=True, stop=True)
            pg.op("dve", "tensor_tensor_scan", rd=(("ps", 6), "segm", "PROD", "PRODa", "PRODb"), wr=("PROD", "PRODa", "PRODb"), out=PROD[:, 128:256],
                  data0=segm[:, 0:128], data1=ps[6][:, 256:384],
                  initial=0.0, op0=ALU.mult, op1=ALU.add)
            pg.op("dve", "tensor_tensor", rd=("PROD", "PRODa", "PRODb"), wr=("BP",), out=BPv[:, :, 0:16], in0=CUM[:, :, 15:16].to_broadcast([P, H, 16]), in1=CUM[:, :, :],
                  op=ALU.subtract)
            pg.op("dve", "tensor_tensor", rd=(("ps", 6), "BP"), wr=("BP",), out=BPv[:, :, 0:16], in0=ps[6][:, 128:256].rearrange("p (h j) -> p h j", h=H),
                  in1=BPv[:, :, 0:16], op=ALU.add)
            pg.op("dve", "tensor_copy", rd=("segm", "BP"), wr=("BP",), out=BPv[:, :, 16:17], in_=segm[:, 136:144].unsqueeze(2))
            pg.op("dve", "tensor_copy", rd=("NLF1", "BP"), wr=("BP",), out=BPv[0:1, :, 16:17], in_=NLF1[0:1, :].unsqueeze(2))
            yield
            for j in range(PAGES + 1):
                slot = pcount % NSLOT
                pcount += 1
                if j < PAGES:
                    col = b * PAGES + j
                    pg.dma("pool", "s_kb%d" % slot, rd=("IDX",), wr=(("KB", slot),), meth="indirect_dma_start", out=KBs[slot][:, :], out_offset=None,
                           in_=cache_k[:, :], in_offset=bass.IndirectOffsetOnAxis(ap=IDX[:, col:col + 1], axis=0))
                    pg.dma("pool", "s_vb%d" % slot, rd=("IDX",), wr=(("VB", slot),), meth="indirect_dma_start", out=VBs[slot][:, :], out_offset=None,
                           in_=cache_v[:, :], in_offset=bass.IndirectOffsetOnAxis(ap=IDX[:, col:col + 1], axis=0))
                else:
                    pg.dma("pool", "s_kb%d" % slot, rd=("ksd",), wr=(("KB", slot),), out=KBs[slot][0:1, :], in_=k_smp[b:b + 1, :])
                    pg.dma("pool", "s_vb%d" % slot, rd=("vsd",), wr=(("VB", slot),), out=VBs[slot][0:1, :], in_=v_smp[b:b + 1, :])
                pg.op("dve", "tensor_tensor", rd=(("KB", slot), "QB", "PROD", "PRODa", "PRODb"), wr=("PRODa", "PRODb"), out=PRODb[:, :], in0=KBs[slot][:, :],
                      in1=QB[:, :], op=ALU.mult)
                pg.op("dve", "tensor_reduce", rd=("PRODa",), wr=("TJa",), out=TJ[:, 0:4], in_=PRODb[:, 0:512].rearrange("p (h d) -> p h d", h=4),
                      axis=AX.X, op=ALU.add)
                for hh in range(4):
                    pg.op("act", "activation", rd=("PRODb",), wr=("PRODb", "TJb"), out=PRODb[:, 512 + hh * 128:512 + (hh + 1) * 128],
                          in_=PRODb[:, 512 + hh * 128:512 + (hh + 1) * 128], func=AF.Copy, accum_out=TJ[:, 4 + hh:5 + hh])
                pg.op("dve", "scalar_tensor_tensor", rd=("TJa", "TJb", "BP"), wr=("TJ", "TJa", "TJb"), out=TJ[:, 0:H].unsqueeze(2), in0=TJ[:, 0:H].unsqueeze(2), scalar=SCALE,
                      in1=BPv[:, :, j:j + 1], op0=ALU.mult, op1=ALU.add)
                pg.op("act", "activation", rd=("TJ",), wr=(("PB", j),), out=PBf[:, :, j:j + 1], in_=TJ[:, 0:H].unsqueeze(2), func=AF.Exp)
                first, last = (j == 0), (j == PAGES)
                for h in range(H):
                    bank = 4 if h < 4 else 5
                    pg.op("pe", "matmul", rd=(("PB", j), ("VB", slot)), wr=(("ps", bank),), out=ps[bank][0:1, (h % 4) * 128:(h % 4 + 1) * 128],
                          lhsT=PBf[:, h, j:j + 1], rhs=VBs[slot][:, h * 128:(h + 1) * 128], start=(first and h % 4 == 0), stop=last, skip_group_check=True)
                pg.op("pe", "matmul", rd=(("PB", j), "onesb"), wr=(("ps", 6),), out=ps[6][0:1, 384:392], lhsT=onesb[:, 0:1],
                      rhs=PBf[:, :, j:j + 1].rearrange("p h o -> p (h o)"), start=first, stop=last)
                yield
            pg.op("dve", "reciprocal", rd=(("ps", 6),), wr=("ONs",), out=ONs[0:1, 0:H], in_=ps[6][0:1, 384:392])
            for hb, bank in enumerate([4, 5]):
                pg.op("dve", "tensor_tensor", rd=(("ps", bank), "ONs", "PROD", "PRODa", "PRODb"), wr=("PROD", "PRODa", "PRODb"), out=OSm[0:1, :].rearrange("p (h d) -> p h d", h=4),
                      in0=ps[bank][0:1, :].rearrange("p (h d) -> p h d", h=4), in1=ONs[0:1, hb * 4:hb * 4 + 4].unsqueeze(2).to_broadcast([1, 4, 128]),
                      op=ALU.mult)
                for hh in range(4):
                    h = hb * 4 + hh
                    pg.op("pe", "transpose", rd=("PROD", "PRODa", "PRODb", "cident"), wr=(("ps", 6),), out=ps[6][:, 392 + h:393 + h], in_=OSm[0:1, hh * 128:(hh + 1) * 128],
                          identity=cident[0:1, 0:1])
            pg.op("act", "activation", rd=(("ps", 6),), wr=tuple(("QA", h) for h in range(H)), out=QA[:, :, T + b:T + b + 1],
                  in_=ps[6][:, 392:400].unsqueeze(2), func=AF.Copy)
            yield

    sgen = {"g": None, "done": False}

    def pump(n):
        if not (with_samples and with_sattn) or sgen["done"]:
            return
        if sgen["g"] is None:
            sgen["g"] = sample_gen()
        for _ in range(n):
            try:
                next(sgen["g"])
            except StopIteration:
                sgen["done"] = True
                return

    for t in range(NT):
        load_norm_tile(x_pre[t * 128:(t + 1) * 128, :], 128, t % 2, xnT, t * 128, 64, "xnT")
    pg.cut("P1n")
    for s in range(4):
        slot, (wv,) = slab_get(SL["p1k"][s])
        for cc in range(2):
            h = s * 2 + cc
            for g in range(2):
                bank = psum_next(B4)
                mm_fm(bank, wv, cc, xnT, KC, g * 512, 512, ("xnT",), slot)
                evac(alt(), KT[:, h, g * 512:(g + 1) * 512], ps[bank][:, 0:512], rd=(("ps", bank),), wr=(("KT", h),))
    for s in range(4):
        slot, (wv,) = slab_get(SL["p1v"][s])
        for t in range(NT):
            bank = psum_next(B4)
            mm_tm(bank, wv, xnT, KC, t * 128, 128, 256, ("xnT",), slot)
            evac(alt(), Vt[:, t, s * 256:(s + 1) * 256], ps[bank][:, 0:256], rd=(("ps", bank),), wr=(("V", t),))
    pg.cut("P1kv")
    slot, (wv,) = slab_get(SL["p1f"])
    bank = psum_next(B4)
    for t in range(NT):
        for kc in range(KC):
            pg.op("pe", "matmul", rd=(("wb", slot), "xnT"), wr=(("ps", bank),), out=ps[bank][:, t * 8:t * 8 + 8],
                  lhsT=xnT[:, kc, t * 128:(t + 1) * 128], rhs=wv[:, kc, 0:8], start=(kc == 0), stop=(kc == KC - 1))
    logf_from_psum(bank, 128, NT, LF[:, 0:NT, :].rearrange("p t h -> p (t h)"), "LFa")
    pg.cut("P1f")
    for s in range(4):
        slot, (wv,) = slab_get(SL["p1x"][s])
        for cc in range(2):
            n = s * 2 + cc
            banks = [0, 1] if n % 2 == 0 else [2, 3]
            for g in range(2):
                mm_fm(banks[g], wv, cc, xnT, KC, g * 512, 512, ("xnT",), slot)
            rnn_chunk(n, banks, False)

    pg.cut("P1")
    for t in range(NT):
        load_norm_tile(x_own[t * 128:(t + 1) * 128, :], 128, t % 2, xnT, t * 128, 64, "xnT")
    pg.cut("P2n0")
    load_norm_tile(x_smp[:, :], NS, 0, xnT, T, 64, "xnT")
    pg.cut("P2n")

    for s in range(4):
        slot, (wv,) = slab_get(SL["p2q"][s])
        for cc in range(2):
            h = s * 2 + cc
            for (c0, n) in GROUPS:
                bank = psum_next(B4)
                mm_fm(bank, wv, cc, xnT, KC, c0, n, ("xnT",), slot)
                evac(alt(), QA[:, h, c0:c0 + n], ps[bank][:, 0:n], rd=(("ps", bank),), wr=(("QA", h),))
        bank = psum_next(B4)
        mm_tm(bank, wv, xnT, KC, T, NS, 256, ("xnT",), slot)
        stage_out(bank, NS, 256, scr_q[:, s * 256:(s + 1) * 256], "s_oq", SST, "s", wr=("scr_q",))
    pg.cut("P2q")
    for s in range(4):
        slot, (wv,) = slab_get(SL["p2k"][s])
        for cc in range(2):
            h = s * 2 + cc
            for g in range(2):
                bank = psum_next(B4)
                mm_fm(bank, wv, cc, xnT, KC, g * 512, 512, ("xnT",), slot)
                evac(alt(), KT[:, h, T + g * 512:T + (g + 1) * 512], ps[bank][:, 0:512], rd=(("ps", bank),), wr=(("KT", h),))
        for t in range(NT):
            bank = psum_next(B4)
            mm_tm(bank, wv, xnT, KC, t * 128, 128, 256, ("xnT",), slot)
            stage_out(bank, 128, 256, k_own[t * 128:(t + 1) * 128, s * 256:(s + 1) * 256], "s_ok", OST, "o")
        bank = psum_next(B4)
        mm_tm(bank, wv, xnT, KC, T, NS, 256, ("xnT",), slot)
        stage_out(bank, NS, 256, k_smp[:, s * 256:(s + 1) * 256], "s_oks", SST, "s", wr=("ksd",))
    pg.cut("P2k")
    for s in range(4):
        slot, (wv,) = slab_get(SL["p2v"][s])
        for t in range(NT):
            bank = psum_next(B4)
            mm_tm(bank, wv, xnT, KC, t * 128, 128, 256, ("xnT",), slot)
            if dbg != 1:
                evac("act", Vt[:, NT + t, s * 256:(s + 1) * 256], ps[bank][:, 0:256], rd=(("ps", bank),), wr=(("V", NT + t),))
            if dbg != 2:
                vdst = k_own if dbg == 3 else v_own
                stage_out(bank, 128, 256, vdst[t * 128:(t + 1) * 128, s * 256:(s + 1) * 256], "s_ov", OST, "o")
        bank = psum_next(B4)
        mm_tm(bank, wv, xnT, KC, T, NS, 256, ("xnT",), slot)
        stage_out(bank, NS, 256, v_smp[:, s * 256:(s + 1) * 256], "s_oks", SST, "s", wr=("vsd",))
    pg.cut("P2v")
    slot, (wv,) = slab_get(SL["p2f"])
    bank = psum_next(B4)
    for t in range(NT):
        for kc in range(KC):
            pg.op("pe", "matmul", rd=(("wb", slot), "xnT"), wr=(("ps", bank),), out=ps[bank][:, t * 8:t * 8 + 8],
                  lhsT=xnT[:, kc, t * 128:(t + 1) * 128], rhs=wv[:, kc, 0:8], start=(kc == 0), stop=(kc == KC - 1))
    logf_from_psum(bank, 128, NT, LF[:, NT:2 * NT, :].rearrange("p t h -> p (t h)"), "LFb")
    pg.dma("sp", "s_ol", K=4, rd=("LFb",), out=lf_own.rearrange("(t p) h -> p t h", p=P), in_=LF[:, NT:2 * NT, :])
    bank = psum_next(B4)
    for kc in range(KC):
        pg.op("pe", "matmul", rd=(("wb", slot), "xnT"), wr=(("ps", bank),), out=ps[bank][0:NS, 0:8], lhsT=xnT[:, kc, T:T + NS],
              rhs=wv[:, kc, 0:8], start=(kc == 0), stop=(kc == KC - 1))
    logf_from_psum(bank, NS, 1, LFS[0:NS, 0:8], "LFS")
    pg.dma("sp", "s_ol", K=4, rd=("LFS",), out=lf_smp[:, :], in_=LFS[0:NS, 0:8])
    pg.op("dve", "tensor_scalar", rd=("LFS",), wr=("LFS2",), out=LFS[0:NS, 8:16], in0=LFS[0:NS, 0:8], scalar1=-1.0, scalar2=None, op0=ALU.mult)
    pg.dma("sp", "s_oq2", rd=("LFS2",), wr=("scr_nlf",), out=scr_nlf[:, :], in_=LFS[0:NS, 8:16])

    pg.cut("P2a")
    for t in range(2 * NT):
        bank = psum_next(B4)
        for j in range(t + 1):
            lhs = cutri if j == t else conesf
            pg.op("pe", "matmul", rd=("LFa", "LFb", "cutri", "conesf"), wr=(("ps", bank),), out=ps[bank][:, 0:8], lhsT=lhs[:, :],
                  rhs=LF[:, j, :], start=(j == 0), stop=(j == t))
        pg.op("act", "activation", rd=(("ps", bank),), wr=("CTk",), out=CTk[:, t, :], in_=ps[bank][:, 0:8], func=AF.Copy)
    pg.op("dve", "tensor_scalar", rd=("CTk", "flag"), wr=("BKb",), out=BKb[:, 0:NT, :].rearrange("p t h -> p (t h)"),
          in0=CTk[:, 0:NT, :].rearrange("p t h -> p (t h)"), scalar1=-1.0, scalar2=flag[:, 1:2], op0=ALU.mult, op1=ALU.add)
    pg.op("dve", "tensor_scalar", rd=("CTk",), wr=("BKb",), out=BKb[:, NT:, :].rearrange("p t h -> p (t h)"),
          in0=CTk[:, NT:, :].rearrange("p t h -> p (t h)"), scalar1=-1.0, scalar2=None, op0=ALU.mult)

    pg.cut("P2b")
    for n in range(RC):
        slot, (wvx, wvy) = slab_get(SL["p2x"][n])
        for g in range(2):
            mm_fm(g, wvx, 0, xnT, KC, g * 512, 512, ("xnT",), slot)
        for g in range(2):
            mm_fm(2 + g, wvy, 0, xnT, KC, g * 512, 512, ("xnT",), slot)
        mm_fm(7, wvx, 0, xnT, KC, T, NS, ("xnT",), slot)
        evac("dve", XRS[:, n, 3 * NS:4 * NS], ps[7][:, 0:NS], rd=(("ps", 7),), wr=(("XRS", n),))
        mm_fm(7, wvy, 0, xnT, KC, T, NS, ("xnT",), slot)
        evac("dve", YRS[:, n, :], ps[7][:, 0:NS], rd=(("ps", 7),), wr=(("YRS", n),))
        rnn_chunk(n, [0, 1], True, yr_banks=[2, 3], gbanks=(7, 7))
        pump(9)
    pg.op("pe", "transpose", rd=("hl", "cident"), wr=(("ps", 0),), out=ps[0][0:RC, 0:128], in_=hl[:, 0:RC], identity=cident[:, :])
    pg.op("dve", "tensor_copy", rd=(("ps", 0),), wr=(("stg", "o", 0),), out=OST[0][0:RC, 0:128], in_=ps[0][0:RC, 0:128])
    pg.dma("sp", "s_ol", K=4, rd=(("stg", "o", 0),), wr=("hlTd",), out=h_own[:, :], in_=OST[0][0:RC, 0:128])
    pg.op("pe", "transpose", rd=("cvo", "cident"), wr=(("ps", 1),), out=ps[1][0:3 * RC, 0:128], in_=cvo[:, 0:3 * RC], identity=cident[:, :])
    pg.op("dve", "tensor_copy", rd=(("ps", 1),), wr=(("stg", "o", 1),), out=OST[1][0:3 * RC, 0:128], in_=ps[1][0:3 * RC, 0:128])
    pg.dma("sp", "s_ol", K=4, rd=(("stg", "o", 1),), wr=("cvTd",), out=conv_own[:, :], in_=OST[1][0:3 * RC, 0:128])

    pg.cut("P2c")
    if with_samples:
        SJ = aview(O_JNK, 4096, F32)
        pg.dma("sp", "s_sj", wr=("JNK",), out=SJ[0:NS, :], in_=state_h[:, :])
        for n in range(RC):
            pg.op("pe", "transpose", rd=("JNK", "cident"), wr=(("ps", 1),), out=ps[1][:, n * NS:(n + 1) * NS],
                  in_=SJ[0:NS, n * 128:(n + 1) * 128], identity=cident[0:NS, 0:NS])
        pg.op("dve", "tensor_copy", rd=(("ps", 1),), wr=("H0S",), out=H0S[:, :], in_=ps[1][:, 0:RC * NS])
        for i in range(3):
            pg.dma("sp", "s_sj", wr=("JNK",), out=SJ[0:NS, :], in_=state_conv[:, i * 1024:(i + 1) * 1024])
            for n in range(RC):
                pg.op("pe", "transpose", rd=("JNK", "cident"), wr=(("ps", 0),), out=ps[0][:, (n * 4 + i) * NS:(n * 4 + i + 1) * NS],
                      in_=SJ[0:NS, n * 128:(n + 1) * 128], identity=cident[0:NS, 0:NS])
        pg.op("dve", "tensor_copy", rd=(("ps", 0),), wr=tuple(("XRS", n) for n in range(RC)), out=XRS[:, :, 0:3 * NS],
              in_=ps[0][:, 0:RC * 4 * NS].rearrange("p (n c) -> p n c", n=RC)[:, :, 0:3 * NS])
        for n in range(RC):
            cw = lambda i: rp[:, i * 8 + n:i * 8 + n + 1]
            xres = (("XRS", n), "rp")
            pg.op("dve", "tensor_scalar", rd=xres, wr=("CV",), out=CV[:, 0:NS], in0=XRS[:, n, 0:NS], scalar1=cw(0),
                  scalar2=rp[:, 32 + n:33 + n], op0=ALU.mult, op1=ALU.add)
            for i in range(1, 4):
                pg.op("dve", "scalar_tensor_tensor", rd=xres + ("CV",), wr=("CV",), out=CV[:, 0:NS], in0=XRS[:, n, i * NS:(i + 1) * NS],
                      scalar=cw(i), in1=CV[:, 0:NS], op0=ALU.mult, op1=ALU.add)
            pg.op("act", "activation", rd=("CV",), wr=("CVB",), out=CVB[:, 0:NS], in_=CV[:, 0:NS], func=AF.Copy)
            pg.op("pe", "matmul", rd=("CVB", "wrg"), wr=(("ps", 2),), out=ps[2][:, 0:NS], lhsT=wrg[:, n * 128:(n + 1) * 128],
                  rhs=CVB[:, 0:NS], start=True, stop=True)
            pg.op("pe", "matmul", rd=("CVB", "wig"), wr=(("ps", 3),), out=ps[3][:, 0:NS], lhsT=wig[:, n * 128:(n + 1) * 128],
                  rhs=CVB[:, 0:NS], start=True, stop=True)
            pg.op("act", "activation", rd=(("ps", 2), "rp"), wr=("TA",), out=TA[:, 0:NS], in_=ps[2][:, 0:NS], func=AF.Sigmoid,
                  bias=rp[:, 40 + n:41 + n])
            pg.op("act", "activation", rd=(("ps", 3), "rp"), wr=("TB",), out=TB[:, 0:NS], in_=ps[3][:, 0:NS], func=AF.Sigmoid,
                  bias=rp[:, 48 + n:49 + n])
            pg.op("act", "activation", rd=("TA", "nsp"), wr=("TA",), out=TA[:, 0:NS], in_=TA[:, 0:NS], func=AF.Exp, scale=nsp[:, n:n + 1])
            pg.op("dve", "tensor_tensor", rd=("TA",), wr=("TC",), out=TC[:, 0:NS], in0=TA[:, 0:NS], in1=TA[:, 0:NS], op=ALU.mult)
            pg.op("act", "activation", rd=("TC",), wr=("TC",), out=TC[:, 0:NS], in_=TC[:, 0:NS], func=AF.Sqrt, scale=-1.0, bias=1.0)
            pg.op("dve", "tensor_tensor", rd=("TB", "CV"), wr=("TB",), out=TB[:, 0:NS], in0=TB[:, 0:NS], in1=CV[:, 0:NS], op=ALU.mult)
            pg.op("dve", "tensor_tensor", rd=("TB", "TC"), wr=("TB",), out=TB[:, 0:NS], in0=TB[:, 0:NS], in1=TC[:, 0:NS], op=ALU.mult)
            pg.op("dve", "tensor_tensor", rd=("TA", "H0S"), wr=("HS",), out=HS[:, 0:NS], in0=TA[:, 0:NS], in1=H0S[:, n * NS:(n + 1) * NS], op=ALU.mult)
            pg.op("dve", "tensor_tensor", rd=("HS", "TB"), wr=("HS",), out=HS[:, 0:NS], in0=HS[:, 0:NS], in1=TB[:, 0:NS], op=ALU.add)
            pg.op("dve", "tensor_copy", rd=("HS",), wr=("HNS",), out=HNS[:, n * NS:(n + 1) * NS], in_=HS[:, 0:NS])
            gelu_mul(n, NS, YRS[:, n, :], (("YRS", n),), rnnT[:, n, T:T + NS], ("rnnT", n))
        for n in range(RC):
            bank = 0 if n < 4 else 1
            pg.op("pe", "transpose", rd=("HNS", "cident"), wr=(("ps", bank),), out=ps[bank][0:NS, (n % 4) * 128:(n % 4 + 1) * 128],
                  in_=HNS[:, n * NS:(n + 1) * NS], identity=cident[:, :])
        for k, bank in enumerate([0, 1]):
            pg.op("act", "activation", rd=(("ps", bank),), wr=("JNK",), out=SJ[0:NS, k * 512:(k + 1) * 512], in_=ps[bank][0:NS, 0:512], func=AF.Copy)
        pg.dma("sp", "s_ol", K=4, rd=("JNK",), out=h_smp[:, :], in_=SJ[0:NS, :])
        for n in range(RC):
            bank = 2 if n < 4 else 3
            pg.op("pe", "transpose", rd=(("XRS", n), "cident"), wr=(("ps", bank),), out=ps[bank][0:NS, (n % 4) * 128:(n % 4 + 1) * 128],
                  in_=XRS[:, n, 3 * NS:4 * NS], identity=cident[:, :])
        for k, bank in enumerate([2, 3]):
            pg.op("act", "activation", rd=(("ps", bank),), wr=("JNK",), out=SJ[0:NS, k * 512:(k + 1) * 512], in_=ps[bank][0:NS, 0:512], func=AF.Copy)
        pg.dma("sp", "s_ol", K=4, rd=("JNK",), out=conv_smp[:, 2048:3072], in_=SJ[0:NS, :])
        pg.dma("sp", "s_ol", K=4, rd=(), out=conv_smp[:, 0:2048], in_=state_conv[:, 1024:3072])

    pg.cut("P2d")
    pg.barrier()
    CS = aview(O_RT, 4096, F32)
    R1 = aview(O_RT + 4096, 4096, F32)
    HI = aview(O_RT + 8192, 2048, BF16)
    LO = aview(O_RT + 10240, 2048, BF16)
    for t in range(NT):
        bank = 0 if t < 4 else 1
        pg.op("pe", "transpose", rd=("CTk", "cident"), wr=(("ps", bank),), out=ps[bank][0:H, (t % 4) * 128:(t % 4) * 128 + 128],
              in_=CTk[:, NT + t, :], identity=cident[:, :])
    for hb in range(2):
        pg.op("dve", "tensor_scalar", rd=(("ps", hb),), wr=("CS",), out=CS[0:H, hb * 512:(hb + 1) * 512], in0=ps[hb][0:H, 0:512],
              scalar1=SQD, scalar2=None, op0=ALU.mult)
    pg.op("dve", "tensor_copy", rd=("CS",), wr=("C3",), out=C3[0:H, :], in_=CS[0:H, :])
    pg.op("dve", "tensor_tensor", rd=("CS", "C3"), wr=("R1",), out=R1[0:H, :], in0=CS[0:H, :], in1=C3[0:H, :], op=ALU.subtract)
    pg.op("dve", "tensor_copy", rd=("R1",), wr=("HI",), out=HI[0:H, :], in_=R1[0:H, :])
    pg.dma("sp", "s_c3", K=2, rd=("HI",), wr=("C3",), out=C3[8:16, :], in_=HI[0:H, :])
    pg.op("dve", "tensor_tensor", rd=("R1", "HI"), wr=("CS",), out=CS[0:H, :], in0=R1[0:H, :], in1=HI[0:H, :], op=ALU.subtract)
    pg.op("dve", "tensor_copy", rd=("CS",), wr=("LO",), out=LO[0:H, :], in_=CS[0:H, :])
    pg.dma("sp", "s_c3", K=2, rd=("LO",), wr=("C3",), out=C3[16:24, :], in_=LO[0:H, :])
    pg.barrier()

    pg.cut("P2")

    PT = [aview(O_RT + i * 1024, 1024, BF16) for i in range(3)]
    RD = aview(O_RT + 3072, 1024, F32)
    ptc = {"i": 0}
    sbank = {"i": 0}

    def attn_block(h, qt):
        q0 = qt * 512
        jobs = []
        for j in range(2 * NT):
            if j < NT:
                jobs.append((j, q0, False))
            else:
                k0 = (j - NT) * 128
                if k0 >= q0 + 512:
                    continue
                jobs.append((j, max(q0, k0), k0 >= q0))

        def emit_S(job):
            j, qlo, diag = job
            n = q0 + 512 - qlo
            bank = sbank["i"] % 2
            sbank["i"] += 1
            pg.op("pe", "matmul", rd=(("KT", h), ("QA", h)), wr=(("ps", bank),), out=ps[bank][:, 0:n], lhsT=KT[:, h, j * 128:(j + 1) * 128],
                  rhs=QA[:, h, qlo:qlo + n], start=True, stop=False)
            pg.op("pe", "matmul", rd=("sel3", "C3"), wr=(("ps", bank),), out=ps[bank][:, 0:n], lhsT=sel3[0:24, h * 128:(h + 1) * 128],
                  rhs=C3[0:24, qlo:qlo + n], start=False, stop=not diag)
            if diag:
                pg.op("pe", "matmul", rd=("identb", "ubias"), wr=(("ps", bank),), out=ps[bank][:, 0:128], lhsT=identb[:, :], rhs=ubias[:, :],
                      start=False, stop=True)
            pi = ptc["i"] % 3
            ptc["i"] += 1
            pg.op("act", "activation", rd=(("ps", bank), "BKb"), wr=(("PT", pi),), out=PT[pi][:, 0:n], in_=ps[bank][:, 0:n], func=AF.Exp,
                  bias=BKb[:, j, h:h + 1], scale=SCALE)
            return (j, qlo, n, pi)

        def emit_PV(info, first, last):
            j, qlo, n, pi = info
            o = qlo - q0
            pg.op("pe", "matmul", rd=(("V", j), ("PT", pi)), wr=(("ps", 2),), out=ps[2][:, o:o + n], lhsT=Vt[:, j, h * 128:(h + 1) * 128],
                  rhs=PT[pi][:, 0:n], start=first, stop=last)
            pg.op("pe", "matmul", rd=("onesb", ("PT", pi)), wr=(("ps", 3),), out=ps[3][:, o:o + n], lhsT=onesb[:, :], rhs=PT[pi][:, 0:n],
                  start=first, stop=last)
        infos = [emit_S(jobs[0])]
        for idx in range(len(jobs)):
            if idx + 1 < len(jobs):
                infos.append(emit_S(jobs[idx + 1]))
            emit_PV(infos[idx], idx == 0, idx == len(jobs) - 1)
        for hb in range(2):
            pg.op("dve", "reciprocal", rd=(("ps", 3),), wr=("RD",), out=RD[:, 0:256], in_=ps[3][:, hb * 256:(hb + 1) * 256])
            pg.op("dve", "tensor_tensor", rd=(("ps", 2), "RD"), wr=(("QA", h),), out=QA[:, h, q0 + hb * 256:q0 + (hb + 1) * 256],
                  in0=ps[2][:, hb * 256:(hb + 1) * 256], in1=RD[:, 0:256], op=ALU.mult)

    for h in range(H):
        for qt in range(2):
            attn_block(h, qt)
            pump(9)

    pg.cut("P3")

    SGa = [aview(O_RT + 4096 + i * 2048, 2048, F32) for i in range(3)]
    SGb = [aview(O_RT + 10240 + i * 2048, 2048, F32) for i in range(2)]
    QAall = tuple(("QA", h) for h in range(H))
    RTall = tuple(("rnnT", n) for n in range(RC))
    p4i = {"i": 0}
    GROUPS_P = GROUPS[0:2] if (with_samples and with_sattn) else GROUPS
    for c in range(KC):
        sa, sbk = SL["p4"][c]
        slot_a, (wga, woa) = slab_get(sa)
        for gi, (c0, n) in enumerate(GROUPS_P):
            b0 = psum_next(B4)
            mm_fm(b0, wga, 0, xnT, KC, c0, n, ("xnT",), slot_a)
            pg.op("act", "activation", rd=(("ps", b0),), wr=(("SGa", gi),), out=SGa[gi][:, 0:n], in_=ps[b0][:, 0:n], func=AF.Sigmoid)
            b2 = psum_next(B4)
            mm_fm(b2, woa, 0, QA, 8, c0, n, QAall, slot_a)
            pg.op("dve", "tensor_tensor", rd=(("ps", b2), ("SGa", gi)), wr=(("SGa", gi),), out=SGa[gi][:, 0:n], in0=ps[b2][:, 0:n],
                  in1=SGa[gi][:, 0:n], op=ALU.mult)
        if len(GROUPS_P) == 2:
            b0 = psum_next(B4)
            mm_fm(b0, wga, 0, xnT, KC, T, NS, ("xnT",), slot_a)
            pg.op("act", "activation", rd=(("ps", b0),), wr=("GS",), out=GS[:, c * NS:(c + 1) * NS], in_=ps[b0][:, 0:NS], func=AF.Sigmoid)
        slot_b, (wgr, wor) = slab_get(sbk)
        for gi, (c0, n) in enumerate(GROUPS_P):
            i = p4i["i"] % 2
            p4i["i"] += 1
            b1 = psum_next(B4)
            mm_fm(b1, wgr, 0, xnT, KC, c0, n, ("xnT",), slot_b)
            pg.op("act", "activation", rd=(("ps", b1),), wr=(("SGb", i),), out=SGb[i][:, 0:n], in_=ps[b1][:, 0:n], func=AF.Sigmoid)
            b3 = psum_next(B4)
            mm_fm(b3, wor, 0, rnnT, 8, c0, n, RTall, slot_b)
            pg.op("dve", "tensor_tensor", rd=(("ps", b3), ("SGb", i)), wr=(("SGb", i),), out=SGb[i][:, 0:n], in0=ps[b3][:, 0:n],
                  in1=SGb[i][:, 0:n], op=ALU.mult)
            pg.op("dve", "tensor_tensor", rd=(("SGa", gi), ("SGb", i)), wr=("oT",), out=oT[:, c, c0:c0 + n], in0=SGa[gi][:, 0:n],
                  in1=SGb[i][:, 0:n], op=ALU.add)
        if len(GROUPS_P) == 2:
            b1 = psum_next(B4)
            mm_fm(b1, wgr, 0, xnT, KC, T, NS, ("xnT",), slot_b)
            pg.op("act", "activation", rd=(("ps", b1),), wr=("RRS",), out=RRS[:, c * NS:(c + 1) * NS], in_=ps[b1][:, 0:NS], func=AF.Sigmoid)
            b3 = psum_next(B4)
            mm_fm(b3, wor, 0, rnnT, 8, T, NS, RTall, slot_b)
            pg.op("dve", "tensor_tensor", rd=(("ps", b3), "RRS"), wr=("RRS",), out=RRS[:, c * NS:(c + 1) * NS], in0=ps[b3][:, 0:NS],
                  in1=RRS[:, c * NS:(c + 1) * NS], op=ALU.mult)
        pump(9)
        if 2 <= c < 10:
            kcw = c - 2
            pg.dma("pool", "s_wo", K=4, wr=(("V", 2 * kcw), ("V", 2 * kcw + 1)), out=WOUT[:, kcw, :], in_=w_out[kcw * 128:(kcw + 1) * 128, :])
    pump(10 ** 6)
    if len(GROUPS_P) == 2:
        for s4 in range(4):
            slot, (wv,) = slab_get(SL["p4b"][s4])
            for cc in range(4):
                c = s4 * 4 + cc
                bank = psum_next(B4)
                mm_fm(bank, wv, cc, QA, 8, T, NS, QAall, slot)
                pg.op("dve", "tensor_tensor", rd=(("ps", bank), "GS"), wr=("GS",), out=GS[:, c * NS:(c + 1) * NS], in0=ps[bank][:, 0:NS],
                      in1=GS[:, c * NS:(c + 1) * NS], op=ALU.mult)
                pg.op("dve", "tensor_tensor", rd=("GS", "RRS"), wr=("oT",), out=oT[:, c, T:T + NS], in0=GS[:, c * NS:(c + 1) * NS],
                      in1=RRS[:, c * NS:(c + 1) * NS], op=ALU.add)
    else:
        for s4 in range(4):
            slab_get(SL["p4b"][s4])

    pg.cut("P4")
    pg.barrier()
    for kc in range(8, KC):
        pg.dma("pool", "s_wo", K=4, wr=("WOUT",), out=WOUT[:, kc, :], in_=w_out[kc * 128:(kc + 1) * 128, :])
    pg.dma("sp", "s_g", wr=("GBC",), out=GBC[:, :], in_=g_post_mix[0:1, :].partition_broadcast(P))
    hnT = xnT
    B6 = [0, 1, 2, 3, 4, 5]
    for t in range(NT + 1):
        m = 128 if t < NT else NS
        t0 = t * 128
        xs, hs_ = XT[0], XT[1]
        src = x_own[t0:t0 + 128, :] if t < NT else x_smp[:, :]
        pg.dma("sp", "s_x0", wr=(("XT", 0),), out=xs[0:m, :], in_=src)
        banks = []
        for cg in range(4):
            bank = psum_next(B6)
            banks.append(bank)
            for kc in range(KC):
                pg.op("pe", "matmul", rd=("oT", "WOUT", ("V", 2 * kc), ("V", 2 * kc + 1)) if kc < 8 else ("oT", "WOUT"), wr=(("ps", bank),), out=ps[bank][0:m, 0:512], lhsT=oT[:, kc, t0:t0 + m],
                      rhs=WOUT[:, kc, cg * 512:(cg + 1) * 512], start=(kc == 0), stop=(kc == KC - 1))
        rs, sr = rstd_from([ps[b][0:m, 0:512] for b in banks], m, 2, tuple(("ps", b) for b in banks), nparts=4)
        for cg in range(4):
            b = banks[cg]
            pg.op("dve", "scalar_tensor_tensor", rd=(("ps", b), sr, "GBC", ("XT", 1)), wr=(("XT", 1),), out=hs_[0:m, cg * 512:(cg + 1) * 512],
                  in0=ps[b][0:m, 0:512], scalar=rs, in1=GBC[0:m, cg * 512:(cg + 1) * 512], op0=ALU.mult, op1=ALU.mult)
        pg.op("dve", "tensor_tensor", rd=(("XT", 0), ("XT", 1)), wr=(("XT", 1),), out=hs_[0:m, :], in0=hs_[0:m, :], in1=xs[0:m, :], op=ALU.add)
        dst = y_own[t0:t0 + 128, :] if t < NT else y_smp[:, :]
        pg.dma("sp", "s_h", K=2, rd=(("XT", 1),), wr=(("hd", t),), out=dst, in_=hs_[0:m, :])
        norm_transpose(hs_[0:m, :], ("XT", 1), m, 3, hnT, t0, 80, "hnT")

    pg.cut("P5")
    pg.barrier()
    SGt = [aview(O_XT + i * 2048, 2048, F32) for i in range(2)]
    B8 = [0, 1, 2, 3, 4, 5, 6, 7]
    p6i = {"i": 0}
    for fc in range(FC):
        slot, (wg, wu) = slab_get(SL["p6"][fc])
        for (c0, n) in GROUPS:
            i = p6i["i"] % 2
            p6i["i"] += 1
            bg = psum_next(B8)
            mm_fm(bg, wg, 0, hnT, KC, c0, n, ("hnT",), slot)
            pg.op("act", "activation", rd=(("ps", bg),), wr=(("SGt", i),), out=SGt[i][:, 0:n], in_=ps[bg][:, 0:n], func=AF.Silu)
            bu = psum_next(B8)
            mm_fm(bu, wu, 0, hnT, KC, c0, n, ("hnT",), slot)
            pg.op("dve", "tensor_tensor", rd=(("ps", bu), ("SGt", i)), wr=("actT",), out=actT[:, fc, c0:c0 + n], in0=ps[bu][:, 0:n],
                  in1=SGt[i][:, 0:n], op=ALU.mult)

    pg.cut("P6")
    pg.barrier()
    FTs = [aview(O_OST, 2048, F32), aview(O_SST, 2048, F32)]
    p7i = {"i": 0}
    for c in range(KC):
        sa, sbk = SL["p7"][c]
        slot_a, (wa,) = slab_get(sa)
        gb = []
        for (c0, n) in GROUPS:
            bank = psum_next([0, 1, 2, 3, 4, 5])
            gb.append(bank)
            mm_fm(bank, wa, 0, actT, 22, c0, n, ("actT",), slot_a, first=True, last=False, kbase=0)
        slot_b, (wb_,) = slab_get(sbk)
        for gi, (c0, n) in enumerate(GROUPS):
            bank = gb[gi]
            mm_fm(bank, wb_, 0, actT, 22, c0, n, ("actT",), slot_b, first=False, last=True, kbase=22)
            i = p7i["i"] % 2
            p7i["i"] += 1
            evac("act", FTs[i][:, 0:n], ps[bank][:, 0:n], rd=(("ps", bank),), wr=(("FTs", i),))
            tb = 6 + (i % 2)
            ntile = (n + 127) // 128
            for k in range(ntile):
                mcols = min(128, n - k * 128)
                pg.op("pe", "transpose", rd=(("FTs", i), "cident"), wr=(("ps", tb),), out=ps[tb][0:mcols, k * 128:(k + 1) * 128],
                      in_=FTs[i][:, k * 128:k * 128 + mcols], identity=cident[:, :])
            for k in range(ntile):
                mcols = min(128, n - k * 128)
                t = c0 // 128 + k
                evac(alt(), FFt[t][0:mcols, c * 128:(c + 1) * 128], ps[tb][0:mcols, k * 128:(k + 1) * 128], rd=(("ps", tb),), wr=(("FF", t),))
    pg.dma("sp", "s_g", wr=(("wb", 0),), out=GBC7, in_=g_post_ffn[0:1, :].partition_broadcast(P))
    for t in range(NT + 1):
        m = 128 if t < NT else NS
        t0 = t * 128
        rs, sr = rstd_from([FFt[t][0:m, :]], m, t % 2, (("FF", t),))
        pg.op("dve", "scalar_tensor_tensor", rd=(("FF", t), sr, ("wb", 0)), wr=(("FF", t),), out=FFt[t][0:m, :], in0=FFt[t][0:m, :], scalar=rs,
              in1=GBC7[0:m, :], op0=ALU.mult, op1=ALU.mult)
        hsrc = y_own[t0:t0 + 128, :] if t < NT else y_smp[:, :]
        pg.dma("sp", "s_h7", rd=(("hd", t),), wr=(("wb", 1),), out=HST7[0:m, :], in_=hsrc)
        pg.op("dve", "tensor_tensor", rd=(("FF", t), ("wb", 1)), wr=(("FF", t),), out=FFt[t][0:m, :], in0=FFt[t][0:m, :], in1=HST7[0:m, :], op=ALU.add)
        pg.dma("sp", "s_y", K=4, rd=(("FF", t),), wr=(("hd", t),), out=hsrc, in_=FFt[t][0:m, :])

    pg.enabled = True
    pg.final_wait()
    print("program ops:", {e: len(pg.ops[e]) for e in ENGS}, "sems:", len(pg.sem_names()))
    sems = {name: es.enter_context(nc.semaphore(name)) for name in pg.sem_names()}
    with nc.Block() as block:
        pg.emit(block, sems)
    es.close()
    return nc


def _consts():
    c = np.zeros((P, 6 * 128), np.float32)
    c[:, 0:128] = np.eye(128, dtype=np.float32)
    i = np.arange(128)
    c[:, 128:256] = (i[:, None] <= i[None, :]).astype(np.float32)
    c[:, 256:384] = (i[:, None] > i[None, :]).astype(np.float32)
    c[:, 384:512] = np.where(i[:, None] > i[None, :], NEG, 0.0)
    c[:, 512:640] = 1.0
    sel3 = np.zeros((24, H, 128), np.float32)
    for r in range(24):
        sel3[r, r % 8, :] = 1.0
    seg = np.zeros((P, 2 * H * 17), np.float32)
    sm = np.ones((H, 16), np.float32); sm[:, 0] = 0.0
    seg[:, 0:128] = sm.reshape(1, -1)
    seg[1:, 136:144] = NEG
    bm = np.zeros((H, H, HD), np.float32)
    for h in range(H):
        bm[h, h, :] = 1.0
    return c, sel3.reshape(24, H * 128), seg, bm.reshape(H, H * HD)


_CACHE = {}


def _get_program(n_phys, **kw):
    key = (n_phys, tuple(sorted(kw.items())))
    if key not in _CACHE:
        _CACHE[key] = build_program(n_phys, **kw)
    return _CACHE[key]


def kernel(x_prompt, x_sample, cache_k, cache_v, cache_logf, state_h, state_conv, page_table,
           g_pre_mix, w_in, b_f, conv_w, conv_b, w_rg, b_rg, w_ig, b_ig, lru_lambda,
           w_o_attn, w_o_rnn, w_out, g_post_mix, g_pre_ffn, w_gate, w_up, w_down, g_post_ffn, _cores=None, _opts=None):
    f32 = lambda a: np.ascontiguousarray(np.asarray(a), dtype=np.float32)
    x_prompt = f32(x_prompt); x_sample = f32(x_sample)
    n_phys = int(np.asarray(cache_k).shape[1])
    B, S = x_prompt.shape[0], x_prompt.shape[1]
    xf = x_prompt.reshape(B * S, D)
    xs = x_sample.reshape(-1, D)
    ck = f32(cache_k).reshape(n_phys * 128, H * HD)
    cv = f32(cache_v).reshape(n_phys * 128, H * HD)
    cl = f32(cache_logf).reshape(n_phys, 128 * H)
    sh = f32(state_h).reshape(-1, R)
    sc = f32(state_conv).reshape(-1, 3 * R)
    pt = np.ascontiguousarray(np.asarray(page_table), dtype=np.int32)
    cst, sel3, seg, bm = _consts()
    rnnp = np.concatenate([f32(conv_w).reshape(4 * RC, 128), f32(conv_b).reshape(RC, 128), f32(b_rg).reshape(RC, 128),
                           f32(b_ig).reshape(RC, 128), f32(lru_lambda).reshape(RC, 128)], axis=0)
    shared = dict(
        cst=cst, sel3=sel3, segm=seg, bmask=bm, iota=np.arange(P, dtype=np.float32).reshape(P, 1),
        cache_k=ck, cache_v=cv, cache_lf=cl,
        g_pre_mix=f32(g_pre_mix).reshape(KC, 128), g_pre_ffn=f32(g_pre_ffn).reshape(KC, 128),
        g_post_mix=f32(g_post_mix).reshape(1, D), g_post_ffn=f32(g_post_ffn).reshape(1, D),
        w_in=f32(w_in).reshape(D, IN_COLS), b_f=f32(b_f).reshape(1, H), rnnp=rnnp,
        w_rg=f32(w_rg).reshape(RC, 128, 128), w_ig=f32(w_ig).reshape(RC, 128, 128),
        w_o_attn=f32(w_o_attn).reshape(H * HD, D), w_o_rnn=f32(w_o_rnn).reshape(R, D), w_out=f32(w_out).reshape(D, D),
        w_gate=f32(w_gate).reshape(D, F), w_up=f32(w_up).reshape(D, F), w_down=f32(w_down).reshape(F, D),
    )
    cores = list(range(N_CORES)) if _cores is None else list(_cores)
    in_maps = []
    for c in cores:
        own = xf[c * T:(c + 1) * T]
        odd = (c % 2) == 1
        pre = xf[(c - 1) * T:c * T] if odd else own
        fl = np.zeros((P, 2), np.float32)
        fl[:, 0] = 1.0 if odd else 0.0
        fl[:, 1] = 0.0 if odd else NEG
        ptc = pt[c * NS:(c + 1) * NS]
        m = dict(shared)
        m.update(x_own=own, x_pre=pre, x_smp=xs[c * NS:(c + 1) * NS], flag=fl,
                 state_h=sh[c * NS:(c + 1) * NS], state_conv=sc[c * NS:(c + 1) * NS],
                 ptab=np.ascontiguousarray(ptc.reshape(1, NS * PAGES)), ptab16=np.ascontiguousarray(ptc.T))
        in_maps.append(m)
    nc = _get_program(n_phys, **(_opts or {}))
    res = run_bass_kernel_spmd(nc, in_maps, core_ids=list(range(len(cores)))).results
    nco = len(cores)
    cat = lambda k: np.concatenate([res[i][k] for i in range(nco)], axis=0)
    nb = max(1, nco // 2)
    y_p = cat("y_own").reshape(-1, S, D) if nco % 2 == 0 else cat("y_own")
    y_s = cat("y_smp").reshape(-1, 1, D)
    k_p = cat("k_own").reshape(1, -1, S, H, HD) if nco % 2 == 0 else cat("k_own")
    v_p = cat("v_own").reshape(1, -1, S, H, HD) if nco % 2 == 0 else cat("v_own")
    lf_p = cat("lf_own").reshape(1, -1, S, H) if nco % 2 == 0 else cat("lf_own")
    odd_i = [i for i, c in enumerate(cores) if c % 2 == 1]
    h_p = np.stack([res[i]["h_own"].reshape(R) for i in odd_i])[None] if odd_i else None
    c_p = np.stack([res[i]["conv_own"].reshape(RC, 3, 128).transpose(1, 0, 2).reshape(3, R) for i in odd_i])[None] if odd_i else None
    k_s = cat("k_smp").reshape(1, -1, 1, H, HD)
    v_s = cat("v_smp").reshape(1, -1, 1, H, HD)
    lf_s = cat("lf_smp").reshape(1, -1, 1, H)
    h_s = cat("h_smp").reshape(1, -1, R)
    c_s = cat("conv_smp").reshape(1, -1, 3, R)
    return (y_p, y_s, k_p, v_p, lf_p, h_p, c_p, k_s, v_s, lf_s, h_s, c_s)
```

```python
import numpy as np
import concourse.bass as bass
import concourse.mybir as mybir
from concourse.bass_utils import run_bass_kernel_spmd

F32 = mybir.dt.float32
BF16 = mybir.dt.bfloat16
I32 = mybir.dt.int32
ALU = mybir.AluOpType
AF = mybir.ActivationFunctionType
AX = mybir.AxisListType

P = 128
D = 2048
KC = 16
T = 1024
NT = 8
NS = 16
TT = T + NS
H = 8
HD = 128
R = 1024
RC = 8
F = 5632
FC = 44
PAGES = 16
OFF_Q, OFF_K, OFF_V, OFF_F, OFF_XR, OFF_YR, OFF_GA, OFF_GR = 0, 1024, 2048, 3072, 3080, 4104, 5128, 7176
IN_COLS = 9224
EPS = 1e-6
SCALE = float(HD) ** -0.5
SQD = float(HD) ** 0.5
NEG = -30000.0
N_CORES = 8

ENGS = ("pe", "act", "dve", "pool", "sp")


class Prog:
    def __init__(self, nc):
        self.nc = nc
        self.ops = {e: [] for e in ENGS}
        self.count = {e: 0 for e in ENGS}
        self.dma_count = {}
        self.grp_count = {}
        self.last_w = {}
        self.readers = {}
        self.waited = {e: {} for e in ENGS}
        self.enabled = True
        self.stop_after = None

    def cut(self, name):
        if self.stop_after == name:
            self.enabled = False

    def _deps(self, eng, rd, wr, is_dma):
        deps = []
        own = "E_" + eng
        for r in rd:
            t = self.last_w.get(r)
            if t is not None:
                deps.append(t)
            if isinstance(r, tuple) and r[0] == "ps":
                for t in self.readers.get(r, ()):
                    if t[0] != own:
                        deps.append(t)
        skip_own = (eng == "pe") and not is_dma
        for w in wr:
            t = self.last_w.get(w)
            if t is not None and not (skip_own and t[0] == own):
                deps.append(t)
            for t in self.readers.get(w, ()):
                if not (skip_own and t[0] == own):
                    deps.append(t)
        best = {}
        for s, v in deps:
            if v > best.get(s, 0):
                best[s] = v
        waits = []
        wd = self.waited[eng]
        for s, v in best.items():
            if v > wd.get(s, 0):
                wd[s] = v
                waits.append((s, v))
        return waits

    def _commit(self, tok, rd, wr):
        for r in rd:
            self.readers.setdefault(r, []).append(tok)
        for w in wr:
            self.last_w[w] = tok
            self.readers[w] = []

    def op(self, eng, meth, rd=(), wr=(), **kw):
        if not self.enabled:
            return
        waits = self._deps(eng, rd, wr, False)
        self.count[eng] += 1
        tok = ("E_" + eng, self.count[eng])
        self._commit(tok, rd, wr)
        self.ops[eng].append((waits, (meth, kw), ("E_" + eng, 1)))

    def dma(self, queue, sem, rd=(), wr=(), meth="dma_start", K=1, **kw):
        if not self.enabled:
            return
        n = self.grp_count.get(sem, 0)
        self.grp_count[sem] = n + 1
        phys = "%s_r%d" % (sem, n % K)
        waits = self._deps(queue, rd, wr, True)
        c = self.dma_count.get(phys, 0)
        if c > 0 and 16 * c > self.waited[queue].get(phys, 0):
            self.waited[queue][phys] = 16 * c
            waits.append((phys, 16 * c))
        self.dma_count[phys] = c + 1
        tok = (phys, 16 * (c + 1))
        self._commit(tok, rd, wr)
        self.ops[queue].append((waits, (meth, kw), (phys, 16)))

    def barrier(self, engines=ENGS):
        if not self.enabled:
            return
        toks = [("E_" + e, self.count[e]) for e in ENGS if self.count[e] > 0]
        toks += [(s, 16 * c) for s, c in self.dma_count.items()]
        for e in engines:
            wd = self.waited[e]
            waits = []
            for s, v in toks:
                if s == "E_" + e:
                    continue
                if v > wd.get(s, 0):
                    wd[s] = v
                    waits.append((s, v))
            if waits:
                self.ops[e].append((waits, None, None))

    def final_wait(self):
        wd = self.waited["sp"]
        waits = []
        for s, c in self.dma_count.items():
            if 16 * c > wd.get(s, 0):
                waits.append((s, 16 * c))
        for e in ENGS:
            if e != "sp" and self.count[e] > 0:
                waits.append(("E_" + e, self.count[e]))
        self.ops["sp"].append((waits, None, None))

    def sem_names(self):
        names = ["E_" + e for e in ENGS if self.count[e] > 0]
        names += list(self.dma_count.keys())
        return names

    def emit(self, block, sems):
        def runner(eng_name):
            def run(e):
                for waits, fn, inc in self.ops[eng_name]:
                    for s, v in waits:
                        e.wait_ge(sems[s], v)
                    if fn is not None:
                        ins = getattr(e, fn[0])(**fn[1])
                        ins.then_inc(sems[inc[0]], inc[1])
            return run
        block.tensor(runner("pe"))
        block.scalar(runner("act"))
        block.vector(runner("dve"))
        block.gpsimd(runner("pool"))
        block.sync(runner("sp"))


def build_program(n_phys, with_samples=True, with_sattn=True, stop_after=None, dbg=0):
    nc = bass.Bass("TRN2", target_bir_lowering=False)
    pg = Prog(nc)
    pg.stop_after = stop_after

    def din(name, shape, dt=F32):
        return nc.dram_tensor(name, list(shape), dt, kind="ExternalInput").ap()

    def dout(name, shape, dt=F32):
        return nc.dram_tensor(name, list(shape), dt, kind="ExternalOutput").ap()

    x_own = din("x_own", [T, D])
    x_pre = din("x_pre", [T, D])
    x_smp = din("x_smp", [NS, D])
    flagd = din("flag", [P, 2])
    cst = din("cst", [P, 6 * 128])
    sel3d = din("sel3", [24, H * 128])
    segd = din("segm", [P, 2 * H * 17])
    bmaskd = din("bmask", [H, H * HD])
    iotad = din("iota", [P, 1])
    cache_kv = din("cache_kv", [n_phys * 128, 2 * H * HD])
    cache_lf = din("cache_lf", [n_phys, 128 * H])
    state_h = din("state_h", [NS, R])
    state_conv = din("state_conv", [NS, 3 * R])
    ptab = din("ptab", [1, NS * PAGES], I32)
    ptab16 = din("ptab16", [PAGES, NS], I32)
    g_pre_mix = din("g_pre_mix", [KC, 128])
    g_pre_ffn = din("g_pre_ffn", [KC, 128])
    g_post_mix = din("g_post_mix", [1, D])
    g_post_ffn = din("g_post_ffn", [1, D])
    w_in = din("w_in", [D, IN_COLS])
    b_f = din("b_f", [1, H])
    rnnp = din("rnnp", [64, 128])
    w_rg = din("w_rg", [RC, 128, 128])
    w_ig = din("w_ig", [RC, 128, 128])
    w_o_attn = din("w_o_attn", [H * HD, D])
    w_o_rnn = din("w_o_rnn", [R, D])
    w_out = din("w_out", [D, D])
    w_gate = din("w_gate", [D, F])
    w_up = din("w_up", [D, F])
    w_down = din("w_down", [F, D])

    y_own = dout("y_own", [T, D])
    y_smp = dout("y_smp", [NS, D])
    k_own = dout("k_own", [T, H * HD])
    v_own = dout("v_own", [T, H * HD])
    lf_own = dout("lf_own", [T, H])
    h_own = dout("h_own", [RC, 128])
    conv_own = dout("conv_own", [3 * RC, 128])
    k_smp = dout("k_smp", [NS, H * HD])
    v_smp = dout("v_smp", [NS, H * HD])
    lf_smp = dout("lf_smp", [NS, H])
    h_smp = dout("h_smp", [NS, R])
    conv_smp = dout("conv_smp", [NS, 3 * R])
    scr_q = nc.dram_tensor("scr_q", [NS, H * HD], F32, kind="Internal").ap()
    scr_nlf = nc.dram_tensor("scr_nlf", [NS, H], F32, kind="Internal").ap()

    from contextlib import ExitStack
    es = ExitStack()

    def sb(name, shape, dt=F32):
        return es.enter_context(nc.sbuf_tensor(name, list(shape), dt))

    A_BYTES = 178048
    arena = sb("arena", [P, A_BYTES // 4], F32)

    def aview(off, nbytes, dt):
        v = arena[:, off // 4:(off + nbytes) // 4]
        return v if dt == F32 else v.bitcast(dt)

    O_G1, O_G2, O_G3, O_G4, O_G5, O_RT, O_XT, O_MISC = 0, 33280, 66560, 99328, 115968, 132608, 148992, 165376
    xnT = aview(O_G1, 33280, BF16).rearrange("p (c t) -> p c t", c=KC)
    KT = aview(O_G2, 32768, BF16).rearrange("p (h t) -> p h t", h=H)
    oT = aview(O_G2, 33280, BF16).rearrange("p (c t) -> p c t", c=KC)
    Vt = aview(O_G3, 32768, BF16).rearrange("p (t c) -> p t c", t=2 * NT)
    QA = aview(O_G4, 16640, BF16).rearrange("p (h t) -> p h t", h=H)
    rnnT = aview(O_G5, 16640, BF16).rearrange("p (h t) -> p h t", h=RC)
    XT = [aview(O_XT + i * 8192, 8192, F32) for i in range(2)]
    WOUT = aview(O_G3, 65536, BF16).rearrange("p (k c) -> p k c", k=KC)
    GBC = aview(O_RT, 8192, F32)
    actT = aview(O_G2, 91520, BF16).rearrange("p (f t) -> p f t", f=FC)
    FFt = [aview(i * 8192, 8192, F32) for i in range(4)] + [aview(124800 + i * 8192, 8192, F32) for i in range(5)]
    XR = aview(O_RT, 4128, F32)
    CV = aview(O_RT + 4128, 2048, F32)
    CVB = aview(O_RT + 6176, 1024, BF16)
    TA = aview(O_RT + 7200, 2048, F32)
    TB = aview(O_RT + 9248, 2048, F32)
    TC = aview(O_RT + 11296, 2048, F32)
    HS = aview(O_RT + 13344, 2048, F32)
    mo = [O_MISC]

    def misc(nbytes, dt):
        v = aview(mo[0], nbytes, dt)
        mo[0] += nbytes
        assert mo[0] <= A_BYTES
        return v
    LF = misc(512, F32).rearrange("p (t h) -> p t h", t=2 * NT)
    CTk = misc(512, F32).rearrange("p (t h) -> p t h", t=2 * NT)
    BKb = misc(512, F32).rearrange("p (t h) -> p t h", t=2 * NT)
    ZT = misc(512, F32)
    C3 = misc(2048, BF16)
    O_OST = mo[0]
    OST = [misc(1024, F32) for _ in range(2)]
    O_SST = mo[0]
    SST = [misc(1024, F32) for _ in range(2)]
    O_JNK = mo[0]
    JNK = misc(4096, BF16)
    LFS = misc(64, F32)

    WB = [sb("wb%d" % i, [P, 4096], BF16) for i in range(2)]
    cident = sb("cident", [P, 128], F32)
    cutri = sb("cutri", [P, 128], F32)
    clstr = sb("clstr", [P, 128], F32)
    conesf = sb("conesf", [P, 128], F32)
    identb = sb("identb", [P, 128], BF16)
    ubias = sb("ubias", [P, 128], BF16)
    onesb = sb("onesb", [P, 128], BF16)
    sel3 = sb("sel3s", [P, H * 128], BF16)
    flag = sb("flags", [P, 2], F32)
    rpraw = sb("rpraw", [P, 128], F32)
    rp = sb("rp", [P, 96], F32)
    nsp = sb("nsp", [P, RC], F32)
    wrg = sb("wrg", [P, RC * 128], BF16)
    wig = sb("wig", [P, RC * 128], BF16)
    bfb = sb("bfb", [P, H], F32)
    hmid = sb("hmid", [P, RC], F32)
    hist = sb("hist", [P, RC * 3], F32)
    stat = sb("stat", [P, 16], F32)
    hl = sb("hl", [P, RC], F32)
    cvo = sb("cvo", [P, RC * 3], F32)
    XRS = sb("xrs", [P, RC, 4 * NS], F32)
    YRS = sb("yrs", [P, RC, NS], F32)
    H0S = sb("h0s", [P, RC * NS], F32)
    HNS = sb("hns", [P, RC * NS], F32)
    stat8 = sb("stat8", [P, 16], F32)
    IDX = sb("idx", [P, NS * PAGES], I32)
    PT16 = sb("pt16", [P, NS], I32)
    iot = sb("iot_s", [P, 1], F32)
    segm = sb("segm_s", [P, 2 * H * 17], F32)
    NLF1 = sb("nlf1", [P, H], F32)
    GS = sb("gs_s", [P, KC * NS], F32)
    RRS = sb("rr_s", [P, KC * NS], F32)
    GBC7 = WB[0][:, :].bitcast(F32)
    HST7 = WB[1][:, :].bitcast(F32)

    ps = [es.enter_context(nc.psum_tensor("ps%d" % i, [P, 512], F32)) for i in range(8)]

    def psb(i):
        return ps[i][:, :].bitcast(BF16)

    def ld(dst, src, q="sp", sem="s_c", wr=()):
        pg.dma(q, sem, rd=(), wr=wr, K=4, out=dst, in_=src)

    ld(cident[:], cst[:, 0:128], wr=("cident",))
    ld(cutri[:], cst[:, 128:256], wr=("cutri",))
    ld(clstr[:], cst[:, 256:384], wr=("clstr",))
    ld(conesf[:], cst[:, 512:640], wr=("conesf",))
    ld(identb[:], cst[:, 0:128], q="pool", sem="s_cp", wr=("identb",))
    ld(ubias[:], cst[:, 384:512], q="pool", sem="s_cp", wr=("ubias",))
    ld(onesb[:], cst[:, 512:640], q="pool", sem="s_cp", wr=("onesb",))
    ld(sel3[0:24, :], sel3d[:, :], q="pool", sem="s_cp", wr=("sel3",))
    ld(flag[:], flagd[:, :], wr=("flag",))
    ld(rpraw[0:64, :], rnnp[:, :], wr=("rpraw",))
    ld(rpraw[64:80, :], g_pre_mix[:, :], wr=("rpraw",))
    ld(rpraw[80:96, :], g_pre_ffn[:, :], wr=("rpraw",))
    ld(bfb[:], b_f[0:1, :].partition_broadcast(P), wr=("bfb",))
    ld(wrg[:].rearrange("p (n e) -> p n e", n=RC), w_rg.rearrange("n d e -> d n e"), q="pool", sem="s_cp", wr=("wrg",))
    ld(wig[:].rearrange("p (n e) -> p n e", n=RC), w_ig.rearrange("n d e -> d n e"), q="pool", sem="s_cp", wr=("wig",))
    pg.op("pe", "transpose", rd=("rpraw", "cident"), wr=(("ps", 7),), out=ps[7][:, 0:96], in_=rpraw[0:96, :], identity=cident[0:96, 0:96])
    pg.op("dve", "tensor_copy", rd=(("ps", 7),), wr=("rp",), out=rp[:], in_=ps[7][:, 0:96])
    pg.op("act", "activation", rd=("rp",), wr=("nsp",), out=nsp[:], in_=rp[:, 56:64], func=AF.Exp, scale=-1.0)
    pg.op("act", "activation", rd=("nsp",), wr=("nsp",), out=nsp[:], in_=nsp[:], func=AF.Ln, bias=1.0)
    pg.op("dve", "tensor_scalar", rd=("nsp",), wr=("nsp",), out=nsp[:], in0=nsp[:], scalar1=-8.0, scalar2=None, op0=ALU.mult)

    pg.cut("P0")
    slab_list = []

    def add_slab(parts):
        slab_list.append(parts)
        return len(slab_list) - 1

    def wslab(src2d, col0, ncols):
        return src2d[:, col0:col0 + ncols].rearrange("(k p) c -> p k c", p=P)

    slab_state = {"next": 0}

    def slab_view(slot, off, kn, ncols):
        return WB[slot][:, off:off + kn * ncols].rearrange("p (k c) -> p k c", k=kn)

    def slab_issue(i):
        if i >= len(slab_list) or i < slab_state["next"]:
            return
        assert i == slab_state["next"]
        slot = i % 2
        for pi_, (off, kn, ncols, src) in enumerate(slab_list[i]):
            pg.dma("pool", "s_w%d_%d" % (slot, pi_), wr=(("wb", slot),), out=slab_view(slot, off, kn, ncols), in_=src)
        slab_state["next"] = i + 1

    def slab_get(i):
        slab_issue(i)
        slab_issue(i + 1)
        slot = i % 2
        return slot, [slab_view(slot, off, kn, ncols) for (off, kn, ncols, src) in slab_list[i]]

    SL = {}
    SL["p1x"], SL["p1k"], SL["p1v"] = [], [], []
    for s in range(4):
        SL["p1x"].append(add_slab([(0, KC, 256, wslab(w_in, OFF_XR + 256 * s, 256))]))
        SL["p1k"].append(add_slab([(0, KC, 256, wslab(w_in, OFF_K + 256 * s, 256))]))
        SL["p1v"].append(add_slab([(0, KC, 256, wslab(w_in, OFF_V + 256 * s, 256))]))
    SL["p1f"] = add_slab([(0, KC, 8, wslab(w_in, OFF_F, 8))])
    SL["p2q"] = [add_slab([(0, KC, 256, wslab(w_in, OFF_Q + 256 * s, 256))]) for s in range(4)]
    SL["p2k"] = [add_slab([(0, KC, 256, wslab(w_in, OFF_K + 256 * s, 256))]) for s in range(4)]
    SL["p2v"] = [add_slab([(0, KC, 256, wslab(w_in, OFF_V + 256 * s, 256))]) for s in range(4)]
    SL["p2f"] = add_slab([(0, KC, 8, wslab(w_in, OFF_F, 8))])
    SL["p2x"] = [add_slab([(0, KC, 128, wslab(w_in, OFF_XR + 128 * n, 128)), (2048, KC, 128, wslab(w_in, OFF_YR + 128 * n, 128))])
                 for n in range(RC)]
    SL["p4"] = []
    for c in range(KC):
        a = add_slab([(0, KC, 128, wslab(w_in, OFF_GA + 128 * c, 128)), (2048, 8, 128, wslab(w_o_attn, 128 * c, 128))])
        b = add_slab([(0, KC, 128, wslab(w_in, OFF_GR + 128 * c, 128)), (2048, 8, 128, wslab(w_o_rnn, 128 * c, 128))])
        SL["p4"].append((a, b))
    SL["p4b"] = [add_slab([(0, 8, 512, wslab(w_o_attn, 512 * s4, 512))]) for s4 in range(4)]
    SL["p6"] = [add_slab([(0, KC, 128, wslab(w_gate, 128 * fc, 128)), (2048, KC, 128, wslab(w_up, 128 * fc, 128))])
                for fc in range(FC)]
    SL["p7"] = []
    for c in range(KC):
        a = add_slab([(0, 22, 128, w_down[0:22 * 128, 128 * c:128 * c + 128].rearrange("(k p) c -> p k c", p=P))])
        b = add_slab([(0, 22, 128, w_down[22 * 128:44 * 128, 128 * c:128 * c + 128].rearrange("(k p) c -> p k c", p=P))])
        SL["p7"].append((a, b))

    psrot = {"i": 0}

    def psum_next(banks):
        b = banks[psrot["i"] % len(banks)]
        psrot["i"] += 1
        return b

    evt = {"i": 0}

    def alt():
        evt["i"] += 1
        return "act" if evt["i"] % 2 else "dve"

    def evac(eng, out, in_, rd, wr):
        if eng == "act":
            pg.op("act", "activation", rd=rd, wr=wr, out=out, in_=in_, func=AF.Copy)
        else:
            pg.op("dve", "tensor_copy", rd=rd, wr=wr, out=out, in_=in_)

    def rstd_from(src, m, slot, src_res, nparts=1):
        sr = ("stat", slot)
        c0 = slot * 4
        if nparts == 1:
            pg.op("act", "activation", rd=src_res, wr=("JNK", sr), out=JNK[0:m, :], in_=src[0], func=AF.Square,
                  accum_out=stat[0:m, c0:c0 + 1])
        else:
            for i in range(nparts):
                pg.op("act", "activation", rd=src_res, wr=("JNK", ("st8", slot)), out=JNK[0:m, 0:512], in_=src[i], func=AF.Square,
                      accum_out=stat8[0:m, slot * 4 + i:slot * 4 + i + 1])
            pg.op("dve", "tensor_reduce", rd=(("st8", slot),), wr=(sr,), out=stat[0:m, c0:c0 + 1], in_=stat8[0:m, slot * 4:slot * 4 + 4],
                  axis=AX.X, op=ALU.add)
        pg.op("act", "activation", rd=(sr,), wr=(sr,), out=stat[0:m, c0 + 1:c0 + 2], in_=stat[0:m, c0:c0 + 1], func=AF.Sqrt,
              scale=1.0 / D, bias=EPS)
        pg.op("dve", "reciprocal", rd=(sr,), wr=(sr,), out=stat[0:m, c0 + 2:c0 + 3], in_=stat[0:m, c0 + 1:c0 + 2])
        return stat[0:m, c0 + 2:c0 + 3], sr

    def norm_transpose(xt, xres, m, slot, dst, dst_cols, gcol0, dst_res):
        rs, sr = rstd_from([xt], m, slot, (xres,))
        pg.op("dve", "tensor_scalar", rd=(xres, sr, "JNK"), wr=("JNK",), out=JNK[0:m, :], in0=xt, scalar1=rs, scalar2=None, op0=ALU.mult)
        for half in range(2):
            bank = 6 + half
            pv = psb(bank)
            for j in range(8):
                kc = half * 8 + j
                pg.op("pe", "transpose", rd=("JNK", "identb"), wr=(("ps", bank),), out=pv[:, j * 128:j * 128 + m],
                      in_=JNK[0:m, kc * 128:(kc + 1) * 128], identity=identb[0:m, 0:m])
            src = pv.rearrange("p (c t) -> p c t", c=8)[:, :, 0:m]
            g = rp[:, gcol0 + half * 8:gcol0 + half * 8 + 8].unsqueeze(2).to_broadcast([P, 8, m])
            pg.op("dve", "tensor_tensor", rd=(("ps", bank), "rp"), wr=(dst_res,), out=dst[:, half * 8:half * 8 + 8, dst_cols:dst_cols + m],
                  in0=src, in1=g, op=ALU.mult)

    def load_norm_tile(src_rows, m, slot, dst, dst_cols, gcol0, dst_res):
        xres = ("XT", slot)
        pg.dma("sp", "s_x%d" % slot, wr=(xres,), out=XT[slot][0:m, :], in_=src_rows)
        norm_transpose(XT[slot][0:m, :], xres, m, slot, dst, dst_cols, gcol0, dst_res)

    def mm_fm(bank, wv, cc, act, kn, c0, n, act_res, slot, first=True, last=True, kbase=0):
        for kc in range(kn):
            pg.op("pe", "matmul", rd=(("wb", slot),) + act_res, wr=(("ps", bank),), out=ps[bank][:, 0:n],
                  lhsT=wv[:, kc, cc * 128:(cc + 1) * 128], rhs=act[:, kbase + kc, c0:c0 + n],
                  start=(first and kc == 0), stop=(last and kc == kn - 1))

    def mm_tm(bank, wv, act, kn, t0, m, ncols, act_res, slot, col0=0):
        for kc in range(kn):
            pg.op("pe", "matmul", rd=(("wb", slot),) + act_res, wr=(("ps", bank),), out=ps[bank][0:m, 0:ncols],
                  lhsT=act[:, kc, t0:t0 + m], rhs=wv[:, kc, col0:col0 + ncols], start=(kc == 0), stop=(kc == kn - 1))

    def logf_from_psum(bank, m, ntile, out_view, out_res):
        n = ntile * 8
        z = ZT[0:m, 0:n]
        pg.op("dve", "tensor_tensor", rd=(("ps", bank), "bfb"), wr=("ZT",), out=z.rearrange("p (t h) -> p t h", h=H),
              in0=ps[bank][0:m, 0:n].rearrange("p (t h) -> p t h", h=H),
              in1=bfb[0:m, :].unsqueeze(1).to_broadcast([m, ntile, H]), op=ALU.add)
        pg.op("act", "activation", rd=("ZT",), wr=("ZT",), out=z, in_=z, func=AF.Exp, scale=-1.0)
        pg.op("act", "activation", rd=("ZT",), wr=("ZT",), out=z, in_=z, func=AF.Ln, bias=1.0)
        pg.op("dve", "tensor_scalar", rd=("ZT",), wr=(out_res,), out=out_view, in0=z, scalar1=-1.0, scalar2=None, op0=ALU.mult)

    hook = {"f": None}

    def rnn_core(n, w, xin, init, init_res, hs_out, last_col, gbanks=(4, 5)):
        cw = lambda i: rp[:, i * 8 + n:i * 8 + n + 1]
        xrd = ("XRh", ("XR", 0), ("XR", 1), "rp")
        pg.op("dve", "tensor_scalar", rd=xrd, wr=("CV",), out=CV[:, 0:w], in0=xin[:, 0:w], scalar1=cw(0),
              scalar2=rp[:, 32 + n:33 + n], op0=ALU.mult, op1=ALU.add)
        for i in range(1, 4):
            pg.op("dve", "scalar_tensor_tensor", rd=xrd + ("CV",), wr=("CV",), out=CV[:, 0:w], in0=xin[:, i:i + w],
                  scalar=cw(i), in1=CV[:, 0:w], op0=ALU.mult, op1=ALU.add)
        pg.op("act", "activation", rd=("CV",), wr=("CVB",), out=CVB[:, 0:w], in_=CV[:, 0:w], func=AF.Copy)
        g0, g1 = gbanks
        pg.op("pe", "matmul", rd=("CVB", "wrg"), wr=(("ps", g0),), out=ps[g0][:, 0:w], lhsT=wrg[:, n * 128:(n + 1) * 128],
              rhs=CVB[:, 0:w], start=True, stop=True)
        pg.op("act", "activation", rd=(("ps", g0), "rp"), wr=("TA",), out=TA[:, 0:w], in_=ps[g0][:, 0:w], func=AF.Sigmoid,
              bias=rp[:, 40 + n:41 + n])
        pg.op("pe", "matmul", rd=("CVB", "wig"), wr=(("ps", g1),), out=ps[g1][:, 0:w], lhsT=wig[:, n * 128:(n + 1) * 128],
              rhs=CVB[:, 0:w], start=True, stop=True)
        pg.op("act", "activation", rd=(("ps", g1), "rp"), wr=("TB",), out=TB[:, 0:w], in_=ps[g1][:, 0:w], func=AF.Sigmoid,
              bias=rp[:, 48 + n:49 + n])
        if hook["f"] is not None:
            hook["f"]()
        pg.op("act", "activation", rd=("TA", "nsp"), wr=("TA",), out=TA[:, 0:w], in_=TA[:, 0:w], func=AF.Exp, scale=nsp[:, n:n + 1])
        pg.op("dve", "tensor_tensor", rd=("TA",), wr=("TC",), out=TC[:, 0:w], in0=TA[:, 0:w], in1=TA[:, 0:w], op=ALU.mult)
        pg.op("act", "activation", rd=("TC",), wr=("TC",), out=TC[:, 0:w], in_=TC[:, 0:w], func=AF.Sqrt, scale=-1.0, bias=1.0)
        pg.op("dve", "tensor_tensor", rd=("TB", "CV"), wr=("TB",), out=TB[:, 0:w], in0=TB[:, 0:w], in1=CV[:, 0:w], op=ALU.mult)
        pg.op("dve", "tensor_tensor", rd=("TB", "TC"), wr=("TB",), out=TB[:, 0:w], in0=TB[:, 0:w], in1=TC[:, 0:w], op=ALU.mult)
        pg.op("dve", "tensor_tensor_scan", rd=("TA", "TB") + init_res, wr=("HS",), out=HS[:, 0:w], data0=TA[:, 0:w], data1=TB[:, 0:w],
              initial=init, op0=ALU.mult, op1=ALU.add)

    def gelu_mul(n, w, ysrc, yres, out, out_res):
        pg.op("act", "activation", rd=yres, wr=("TC",), out=TC[:, 0:w], in_=ysrc, func=AF.Square)
        pg.op("dve", "tensor_scalar", rd=("TC",), wr=("TC",), out=TC[:, 0:w], in0=TC[:, 0:w], scalar1=0.044715, scalar2=1.0,
              op0=ALU.mult, op1=ALU.add)
        pg.op("dve", "tensor_tensor", rd=("TC",) + yres, wr=("TC",), out=TC[:, 0:w], in0=ysrc, in1=TC[:, 0:w], op=ALU.mult)
        pg.op("act", "activation", rd=("TC",), wr=("TC",), out=TC[:, 0:w], in_=TC[:, 0:w], func=AF.Sigmoid, scale=1.5957691216)
        pg.op("dve", "tensor_tensor", rd=("TC",) + yres, wr=("TC",), out=TC[:, 0:w], in0=ysrc, in1=TC[:, 0:w], op=ALU.mult)
        pg.op("dve", "tensor_tensor", rd=("TC", "HS"), wr=(out_res,), out=out, in0=TC[:, 0:w], in1=HS[:, 0:w], op=ALU.mult)

    def rnn_chunk(n, banks, own, yr_banks=None, gbanks=(4, 5)):
        if own:
            pg.op("dve", "tensor_copy", rd=("hist",), wr=("XRh",), out=XR[:, 0:3], in_=hist[:, n * 3:n * 3 + 3])
        else:
            pg.op("dve", "memset", wr=("XRh",), ap=XR[:, 0:3], constant=0.0)
        for g in range(2):
            evac("act", XR[:, 3 + g * 512:3 + (g + 1) * 512], ps[banks[g]][:, 0:512], rd=(("ps", banks[g]),), wr=(("XR", g),))
        for g in range(2):
            if g == 0:
                init, ires = (hmid[:, n:n + 1], ("hmid",)) if own else (0.0, ())
            else:
                init, ires = hl[:, n:n + 1], ("hl",)
            rnn_core(n, 512, XR[:, g * 512:g * 512 + 515], init, ires, HS, 511, gbanks=gbanks)
            pg.op("dve", "tensor_copy", rd=("HS",), wr=("hl",), out=hl[:, n:n + 1], in_=HS[:, 511:512])
            if own:
                yb = yr_banks[g]
                gelu_mul(n, 512, ps[yb][:, 0:512], (("ps", yb),), rnnT[:, n, g * 512:(g + 1) * 512], ("rnnT", n))
        if own:
            pg.op("dve", "tensor_copy", rd=(("XR", 1),), wr=("cvo",), out=cvo[:, n * 3:n * 3 + 3], in_=XR[:, 1024:1027])
        else:
            pg.op("dve", "tensor_scalar", rd=("hl", "flag"), wr=("hmid",), out=hmid[:, n:n + 1], in0=hl[:, n:n + 1],
                  scalar1=flag[:, 0:1], scalar2=None, op0=ALU.mult)
            pg.op("dve", "tensor_scalar", rd=(("XR", 1), "flag"), wr=("hist",), out=hist[:, n * 3:n * 3 + 3], in0=XR[:, 1024:1027],
                  scalar1=flag[:, 0:1], scalar2=None, op0=ALU.mult)

    stg = {"i": 0}

    def stage_out(bank, m, ncols, dram_ap, sem, stq, tag, wr=()):
        i = stg["i"] % 2
        stg["i"] += 1
        res = ("stg", tag, i)
        evac(alt(), stq[i][0:m, 0:ncols], ps[bank][0:m, 0:ncols], rd=(("ps", bank),), wr=(res,))
        pg.dma("sp", sem, rd=(res,), wr=wr, K=2, out=dram_ap, in_=stq[i][0:m, 0:ncols])

    B4 = [0, 1, 2, 3]
    GROUPS = [(0, 512), (512, 512), (T, NS)]

    NSLOT = 3
    KVs = [aview(O_XT + i * 4096, 4096, BF16) for i in range(NSLOT)]
    KBs = [kv[:, 0:1024] for kv in KVs]
    VBs = [kv[:, 1024:2048] for kv in KVs]
    PROD = aview(O_XT + 12288, 2048, F32)
    QB = aview(O_XT + 14336, 2048, BF16)
    LFPv = aview(O_JNK, 4096, F32)
    O_SM = O_RT + 15392
    BPv = aview(O_SM, 544, F32).rearrange("p (h j) -> p h j", h=H)
    PBf = aview(O_SM + 544, 272, BF16).rearrange("p (h j) -> p h j", h=H)
    TJ = aview(O_SM + 816, 32, F32)
    ONs = aview(O_SM + 848, 32, F32)
    PRODb = aview(O_XT + 12288, 2048, BF16)
    LFsb = PROD[:, 0:128]
    CUM = PROD[:, 128:256].rearrange("p (h j) -> p h j", h=H)
    OSm = PROD

    def sample_gen():
        pg.dma("sp", "s_sm", wr=("IDX",), K=2, out=IDX[:, :], in_=ptab[0:1, :].partition_broadcast(P))
        pg.dma("sp", "s_sm", wr=("PT16",), K=2, out=PT16[0:PAGES, :], in_=ptab16[:, :])
        pg.dma("sp", "s_sm", wr=("iot",), K=2, out=iot[:, :], in_=iotad[:, :])
        pg.dma("sp", "s_sm", wr=("segm",), K=2, out=segm[:, :], in_=segd[:, :])
        idf = PROD
        pg.op("dve", "tensor_copy", rd=("IDX",), wr=("PROD", "PRODa", "PRODb"), out=idf[:, 0:NS * PAGES], in_=IDX[:, :])
        pg.op("dve", "tensor_scalar", rd=("PROD", "PRODa", "PRODb", "iot"), wr=("PROD", "PRODa", "PRODb"), out=idf[:, 0:NS * PAGES], in0=idf[:, 0:NS * PAGES], scalar1=128.0,
              scalar2=iot[:, 0:1], op0=ALU.mult, op1=ALU.add)
        pg.op("dve", "tensor_copy", rd=("PROD", "PRODa", "PRODb"), wr=("IDX",), out=IDX[:, :], in_=idf[:, 0:NS * PAGES])
        yield
        pcount = 0
        for b in range(NS):
            pg.dma("pool", "s_lfp", rd=("PT16",), wr=("JNK",), meth="indirect_dma_start", out=LFPv[0:PAGES, :], out_offset=None,
                   in_=cache_lf[:, :], in_offset=bass.IndirectOffsetOnAxis(ap=PT16[0:PAGES, b:b + 1], axis=0))
            pg.dma("pool", "s_qb", rd=("scr_q",), wr=("QB",), out=QB[:, :], in_=scr_q[b:b + 1, :].partition_broadcast(P))
            pg.dma("sp", "s_nl", rd=("scr_nlf",), wr=("NLF1",), out=NLF1[0:1, :], in_=scr_nlf[b:b + 1, :])
            lfp3 = LFPv[0:PAGES, :].rearrange("p (r h) -> p h r", h=H)
            for h in range(H):
                pg.op("pe", "transpose", rd=("JNK", "cident"), wr=(("ps", 6),), out=ps[6][:, h * PAGES:(h + 1) * PAGES], in_=lfp3[:, h, :],
                      identity=cident[0:PAGES, 0:PAGES])
            pg.op("act", "activation", rd=(("ps", 6),), wr=("PROD", "PRODa", "PRODb"), out=LFsb, in_=ps[6][:, 0:128], func=AF.Copy)
            pg.op("pe", "matmul", rd=("PROD", "PRODa", "PRODb", "clstr"), wr=(("ps", 6),), out=ps[6][:, 128:256], lhsT=clstr[:, :], rhs=LFsb, start=True, stop=True)
            pg.op("pe", "matmul", rd=("PROD", "PRODa", "PRODb", "conesf"), wr=(("ps", 6),), out=ps[6][:, 256:384], lhsT=conesf[:, :], rhs=LFsb, start=True, stop=True)
            pg.op("dve", "tensor_tensor_scan", rd=(("ps", 6), "segm", "PROD", "PRODa", "PRODb"), wr=("PROD", "PRODa", "PRODb"), out=PROD[:, 128:256],
                  data0=segm[:, 0:128], data1=ps[6][:, 256:384],
                  initial=0.0, op0=ALU.mult, op1=ALU.add)
            pg.op("dve", "tensor_tensor", rd=("PROD", "PRODa", "PRODb"), wr=("BP",), out=BPv[:, :, 0:16], in0=CUM[:, :, 15:16].to_broadcast([P, H, 16]), in1=CUM[:, :, :],
                  op=ALU.subtract)
            pg.op("dve", "tensor_tensor", rd=(("ps", 6), "BP"), wr=("BP",), out=BPv[:, :, 0:16], in0=ps[6][:, 128:256].rearrange("p (h j) -> p h j", h=H),
                  in1=BPv[:, :, 0:16], op=ALU.add)
            pg.op("dve", "tensor_copy", rd=("segm", "BP"), wr=("BP",), out=BPv[:, :, 16:17], in_=segm[:, 136:144].unsqueeze(2))
            pg.op("dve", "tensor_copy", rd=("NLF1", "BP"), wr=("BP",), out=BPv[0:1, :, 16:17], in_=NLF1[0:1, :].unsqueeze(2))
            yield
            for j in range(PAGES + 1):
                slot = pcount % NSLOT
                pcount += 1
                if j < PAGES:
                    col = b * PAGES + j
                    pg.dma("pool", "s_kb%d" % slot, rd=("IDX",), wr=(("KB", slot), ("VB", slot)), meth="indirect_dma_start", out=KVs[slot][:, :], out_offset=None,
                           in_=cache_kv[:, :], in_offset=bass.IndirectOffsetOnAxis(ap=IDX[:, col:col + 1], axis=0))
                else:
                    pg.dma("pool", "s_kb%d" % slot, rd=("ksd",), wr=(("KB", slot),), out=KBs[slot][0:1, :], in_=k_smp[b:b + 1, :])
                    pg.dma("pool", "s_vb%d" % slot, rd=("vsd",), wr=(("VB", slot),), out=VBs[slot][0:1, :], in_=v_smp[b:b + 1, :])
                pg.op("dve", "tensor_tensor", rd=(("KB", slot), "QB", "PROD", "PRODa", "PRODb"), wr=("PRODa", "PRODb"), out=PRODb[:, :], in0=KBs[slot][:, :],
                      in1=QB[:, :], op=ALU.mult)
                pg.op("dve", "tensor_reduce", rd=("PRODa", "PRODb"), wr=("TJa", "TJb"), out=TJ[:, 0:H], in_=PRODb[:, :].rearrange("p (h d) -> p h d", h=H),
                      axis=AX.X, op=ALU.add)
                pg.op("dve", "scalar_tensor_tensor", rd=("TJa", "TJb", "BP"), wr=("TJ", "TJa", "TJb"), out=TJ[:, 0:H].unsqueeze(2), in0=TJ[:, 0:H].unsqueeze(2), scalar=SCALE,
                      in1=BPv[:, :, j:j + 1], op0=ALU.mult, op1=ALU.add)
                pg.op("act", "activation", rd=("TJ", "TJa", "TJb"), wr=(("PB", j),), out=PBf[:, :, j:j + 1], in_=TJ[:, 0:H].unsqueeze(2), func=AF.Exp)
                first, last = (j == 0), (j == PAGES)
                for h in range(H):
                    bank = 4 if h < 4 else 5
                    pg.op("pe", "matmul", rd=(("PB", j), ("VB", slot)), wr=(("ps", bank),), out=ps[bank][0:1, (h % 4) * 128:(h % 4 + 1) * 128],
                          lhsT=PBf[:, h, j:j + 1], rhs=VBs[slot][:, h * 128:(h + 1) * 128], start=(first and h % 4 == 0), stop=last, skip_group_check=True)
                pg.op("pe", "matmul", rd=(("PB", j), "onesb"), wr=(("ps", 6),), out=ps[6][0:1, 384:392], lhsT=onesb[:, 0:1],
                      rhs=PBf[:, :, j:j + 1].rearrange("p h o -> p (h o)"), start=first, stop=last)
                yield
            pg.op("dve", "reciprocal", rd=(("ps", 6),), wr=("ONs",), out=ONs[0:1, 0:H], in_=ps[6][0:1, 384:392])
            for hb, bank in enumerate([4, 5]):
                pg.op("dve", "tensor_tensor", rd=(("ps", bank), "ONs", "PROD", "PRODa", "PRODb"), wr=("PROD", "PRODa", "PRODb"), out=OSm[0:1, :].rearrange("p (h d) -> p h d", h=4),
                      in0=ps[bank][0:1, :].rearrange("p (h d) -> p h d", h=4), in1=ONs[0:1, hb * 4:hb * 4 + 4].unsqueeze(2).to_broadcast([1, 4, 128]),
                      op=ALU.mult)
                for hh in range(4):
                    h = hb * 4 + hh
                    pg.op("pe", "transpose", rd=("PROD", "PRODa", "PRODb", "cident"), wr=(("ps", 6),), out=ps[6][:, 392 + h:393 + h], in_=OSm[0:1, hh * 128:(hh + 1) * 128],
                          identity=cident[0:1, 0:1])
            pg.op("act", "activation", rd=(("ps", 6),), wr=tuple(("QA", h) for h in range(H)), out=QA[:, :, T + b:T + b + 1],
                  in_=ps[6][:, 392:400].unsqueeze(2), func=AF.Copy)
            yield

    sgen = {"g": None, "done": False}

    def pump(n):
        if not (with_samples and with_sattn) or sgen["done"]:
            return
        if sgen["g"] is None:
            sgen["g"] = sample_gen()
        for _ in range(n):
            try:
                next(sgen["g"])
            except StopIteration:
                sgen["done"] = True
                return

    for t in range(NT):
        load_norm_tile(x_pre[t * 128:(t + 1) * 128, :], 128, t % 2, xnT, t * 128, 64, "xnT")
    pg.cut("P1n")
    B67 = [6, 7]

    def p1_fill(s):
        slot, (wv,) = slab_get(SL["p1k"][s])
        for cc in range(2):
            h = s * 2 + cc
            for g in range(2):
                bank = psum_next(B67)
                mm_fm(bank, wv, cc, xnT, KC, g * 512, 512, ("xnT",), slot)
                evac(alt(), KT[:, h, g * 512:(g + 1) * 512], ps[bank][:, 0:512], rd=(("ps", bank),), wr=(("KT", h),))
                yield
        slot, (wv,) = slab_get(SL["p1v"][s])
        for t in range(NT):
            bank = psum_next(B67)
            mm_tm(bank, wv, xnT, KC, t * 128, 128, 256, ("xnT",), slot)
            evac(alt(), Vt[:, t, s * 256:(s + 1) * 256], ps[bank][:, 0:256], rd=(("ps", bank),), wr=(("V", t),))
            yield

    for s in range(4):
        slot, (wv,) = slab_get(SL["p1x"][s])
        for cc in range(2):
            for g in range(2):
                mm_fm(cc * 2 + g, wv, cc, xnT, KC, g * 512, 512, ("xnT",), slot)
        gen = p1_fill(s)

        def filler(gen=gen):
            for _ in range(3):
                try:
                    next(gen)
                except StopIteration:
                    return
        hook["f"] = filler
        for cc in range(2):
            rnn_chunk(s * 2 + cc, [cc * 2, cc * 2 + 1], False)
        hook["f"] = None
        for _ in gen:
            pass
    pg.cut("P1kv")
    slot, (wv,) = slab_get(SL["p1f"])
    bank = psum_next(B4)
    for t in range(NT):
        for kc in range(KC):
            pg.op("pe", "matmul", rd=(("wb", slot), "xnT"), wr=(("ps", bank),), out=ps[bank][:, t * 8:t * 8 + 8],
                  lhsT=xnT[:, kc, t * 128:(t + 1) * 128], rhs=wv[:, kc, 0:8], start=(kc == 0), stop=(kc == KC - 1))
    logf_from_psum(bank, 128, NT, LF[:, 0:NT, :].rearrange("p t h -> p (t h)"), "LFa")
    pg.cut("P1f")
    pg.cut("P1")
    for t in range(NT):
        load_norm_tile(x_own[t * 128:(t + 1) * 128, :], 128, t % 2, xnT, t * 128, 64, "xnT")
    pg.cut("P2n0")
    load_norm_tile(x_smp[:, :], NS, 0, xnT, T, 64, "xnT")
    pg.cut("P2n")

    for s in range(4):
        slot, (wv,) = slab_get(SL["p2q"][s])
        for cc in range(2):
            h = s * 2 + cc
            for (c0, n) in GROUPS:
                bank = psum_next(B4)
                mm_fm(bank, wv, cc, xnT, KC, c0, n, ("xnT",), slot)
                evac(alt(), QA[:, h, c0:c0 + n], ps[bank][:, 0:n], rd=(("ps", bank),), wr=(("QA", h),))
        bank = psum_next(B4)
        mm_tm(bank, wv, xnT, KC, T, NS, 256, ("xnT",), slot)
        stage_out(bank, NS, 256, scr_q[:, s * 256:(s + 1) * 256], "s_oq", SST, "s", wr=("scr_q",))
    pg.cut("P2q")
    for s in range(4):
        slot, (wv,) = slab_get(SL["p2k"][s])
        for cc in range(2):
            h = s * 2 + cc
            for g in range(2):
                bank = psum_next(B4)
                mm_fm(bank, wv, cc, xnT, KC, g * 512, 512, ("xnT",), slot)
                evac(alt(), KT[:, h, T + g * 512:T + (g + 1) * 512], ps[bank][:, 0:512], rd=(("ps", bank),), wr=(("KT", h),))
        for t in range(NT):
            bank = psum_next(B4)
            mm_tm(bank, wv, xnT, KC, t * 128, 128, 256, ("xnT",), slot)
            stage_out(bank, 128, 256, k_own[t * 128:(t + 1) * 128, s * 256:(s + 1) * 256], "s_ok", OST, "o")
        bank = psum_next(B4)
        mm_tm(bank, wv, xnT, KC, T, NS, 256, ("xnT",), slot)
        stage_out(bank, NS, 256, k_smp[:, s * 256:(s + 1) * 256], "s_oks", SST, "s", wr=("ksd",))
    pg.cut("P2k")
    for s in range(4):
        slot, (wv,) = slab_get(SL["p2v"][s])
        for t in range(NT):
            bank = psum_next(B4)
            mm_tm(bank, wv, xnT, KC, t * 128, 128, 256, ("xnT",), slot)
            if dbg != 1:
                evac("act", Vt[:, NT + t, s * 256:(s + 1) * 256], ps[bank][:, 0:256], rd=(("ps", bank),), wr=(("V", NT + t),))
            if dbg != 2:
                vdst = k_own if dbg == 3 else v_own
                stage_out(bank, 128, 256, vdst[t * 128:(t + 1) * 128, s * 256:(s + 1) * 256], "s_ov", OST, "o")
        bank = psum_next(B4)
        mm_tm(bank, wv, xnT, KC, T, NS, 256, ("xnT",), slot)
        stage_out(bank, NS, 256, v_smp[:, s * 256:(s + 1) * 256], "s_oks", SST, "s", wr=("vsd",))
    pg.cut("P2v")
    slot, (wv,) = slab_get(SL["p2f"])
    bank = psum_next(B4)
    for t in range(NT):
        for kc in range(KC):
            pg.op("pe", "matmul", rd=(("wb", slot), "xnT"), wr=(("ps", bank),), out=ps[bank][:, t * 8:t * 8 + 8],
                  lhsT=xnT[:, kc, t * 128:(t + 1) * 128], rhs=wv[:, kc, 0:8], start=(kc == 0), stop=(kc == KC - 1))
    logf_from_psum(bank, 128, NT, LF[:, NT:2 * NT, :].rearrange("p t h -> p (t h)"), "LFb")
    pg.dma("sp", "s_ol", K=4, rd=("LFb",), out=lf_own.rearrange("(t p) h -> p t h", p=P), in_=LF[:, NT:2 * NT, :])
    bank = psum_next(B4)
    for kc in range(KC):
        pg.op("pe", "matmul", rd=(("wb", slot), "xnT"), wr=(("ps", bank),), out=ps[bank][0:NS, 0:8], lhsT=xnT[:, kc, T:T + NS],
              rhs=wv[:, kc, 0:8], start=(kc == 0), stop=(kc == KC - 1))
    logf_from_psum(bank, NS, 1, LFS[0:NS, 0:8], "LFS")
    pg.dma("sp", "s_ol", K=4, rd=("LFS",), out=lf_smp[:, :], in_=LFS[0:NS, 0:8])
    pg.op("dve", "tensor_scalar", rd=("LFS",), wr=("LFS2",), out=LFS[0:NS, 8:16], in0=LFS[0:NS, 0:8], scalar1=-1.0, scalar2=None, op0=ALU.mult)
    pg.dma("sp", "s_oq2", rd=("LFS2",), wr=("scr_nlf",), out=scr_nlf[:, :], in_=LFS[0:NS, 8:16])

    pg.cut("P2a")
    for t in range(2 * NT):
        bank = psum_next(B4)
        for j in range(t + 1):
            lhs = cutri if j == t else conesf
            pg.op("pe", "matmul", rd=("LFa", "LFb", "cutri", "conesf"), wr=(("ps", bank),), out=ps[bank][:, 0:8], lhsT=lhs[:, :],
                  rhs=LF[:, j, :], start=(j == 0), stop=(j == t))
        pg.op("act", "activation", rd=(("ps", bank),), wr=("CTk",), out=CTk[:, t, :], in_=ps[bank][:, 0:8], func=AF.Copy)
    pg.op("dve", "tensor_scalar", rd=("CTk", "flag"), wr=("BKb",), out=BKb[:, 0:NT, :].rearrange("p t h -> p (t h)"),
          in0=CTk[:, 0:NT, :].rearrange("p t h -> p (t h)"), scalar1=-1.0, scalar2=flag[:, 1:2], op0=ALU.mult, op1=ALU.add)
    pg.op("dve", "tensor_scalar", rd=("CTk",), wr=("BKb",), out=BKb[:, NT:, :].rearrange("p t h -> p (t h)"),
          in0=CTk[:, NT:, :].rearrange("p t h -> p (t h)"), scalar1=-1.0, scalar2=None, op0=ALU.mult)

    pg.cut("P2b")
    for n in range(RC):
        slot, (wvx, wvy) = slab_get(SL["p2x"][n])
        for g in range(2):
            mm_fm(g, wvx, 0, xnT, KC, g * 512, 512, ("xnT",), slot)
        for g in range(2):
            mm_fm(2 + g, wvy, 0, xnT, KC, g * 512, 512, ("xnT",), slot)
        mm_fm(7, wvx, 0, xnT, KC, T, NS, ("xnT",), slot)
        evac("dve", XRS[:, n, 3 * NS:4 * NS], ps[7][:, 0:NS], rd=(("ps", 7),), wr=(("XRS", n),))
        mm_fm(7, wvy, 0, xnT, KC, T, NS, ("xnT",), slot)
        evac("dve", YRS[:, n, :], ps[7][:, 0:NS], rd=(("ps", 7),), wr=(("YRS", n),))
        hook["f"] = lambda: pump(2)
        rnn_chunk(n, [0, 1], True, yr_banks=[2, 3], gbanks=(7, 7))
        hook["f"] = None
        pump(5)
    pg.op("pe", "transpose", rd=("hl", "cident"), wr=(("ps", 0),), out=ps[0][0:RC, 0:128], in_=hl[:, 0:RC], identity=cident[:, :])
    pg.op("dve", "tensor_copy", rd=(("ps", 0),), wr=(("stg", "o", 0),), out=OST[0][0:RC, 0:128], in_=ps[0][0:RC, 0:128])
    pg.dma("sp", "s_ol", K=4, rd=(("stg", "o", 0),), wr=("hlTd",), out=h_own[:, :], in_=OST[0][0:RC, 0:128])
    pg.op("pe", "transpose", rd=("cvo", "cident"), wr=(("ps", 1),), out=ps[1][0:3 * RC, 0:128], in_=cvo[:, 0:3 * RC], identity=cident[:, :])
    pg.op("dve", "tensor_copy", rd=(("ps", 1),), wr=(("stg", "o", 1),), out=OST[1][0:3 * RC, 0:128], in_=ps[1][0:3 * RC, 0:128])
    pg.dma("sp", "s_ol", K=4, rd=(("stg", "o", 1),), wr=("cvTd",), out=conv_own[:, :], in_=OST[1][0:3 * RC, 0:128])

    pg.cut("P2c")
    if with_samples:
        SJ = aview(O_JNK, 4096, F32)
        pg.dma("sp", "s_sj", wr=("JNK",), out=SJ[0:NS, :], in_=state_h[:, :])
        for n in range(RC):
            pg.op("pe", "transpose", rd=("JNK", "cident"), wr=(("ps", 1),), out=ps[1][:, n * NS:(n + 1) * NS],
                  in_=SJ[0:NS, n * 128:(n + 1) * 128], identity=cident[0:NS, 0:NS])
        pg.op("dve", "tensor_copy", rd=(("ps", 1),), wr=("H0S",), out=H0S[:, :], in_=ps[1][:, 0:RC * NS])
        for i in range(3):
            pg.dma("sp", "s_sj", wr=("JNK",), out=SJ[0:NS, :], in_=state_conv[:, i * 1024:(i + 1) * 1024])
            for n in range(RC):
                pg.op("pe", "transpose", rd=("JNK", "cident"), wr=(("ps", 0),), out=ps[0][:, (n * 4 + i) * NS:(n * 4 + i + 1) * NS],
                      in_=SJ[0:NS, n * 128:(n + 1) * 128], identity=cident[0:NS, 0:NS])
        pg.op("dve", "tensor_copy", rd=(("ps", 0),), wr=tuple(("XRS", n) for n in range(RC)), out=XRS[:, :, 0:3 * NS],
              in_=ps[0][:, 0:RC * 4 * NS].rearrange("p (n c) -> p n c", n=RC)[:, :, 0:3 * NS])
        for n in range(RC):
            cw = lambda i: rp[:, i * 8 + n:i * 8 + n + 1]
            xres = (("XRS", n), "rp")
            pg.op("dve", "tensor_scalar", rd=xres, wr=("CV",), out=CV[:, 0:NS], in0=XRS[:, n, 0:NS], scalar1=cw(0),
                  scalar2=rp[:, 32 + n:33 + n], op0=ALU.mult, op1=ALU.add)
            for i in range(1, 4):
                pg.op("dve", "scalar_tensor_tensor", rd=xres + ("CV",), wr=("CV",), out=CV[:, 0:NS], in0=XRS[:, n, i * NS:(i + 1) * NS],
                      scalar=cw(i), in1=CV[:, 0:NS], op0=ALU.mult, op1=ALU.add)
            pg.op("act", "activation", rd=("CV",), wr=("CVB",), out=CVB[:, 0:NS], in_=CV[:, 0:NS], func=AF.Copy)
            pg.op("pe", "matmul", rd=("CVB", "wrg"), wr=(("ps", 2),), out=ps[2][:, 0:NS], lhsT=wrg[:, n * 128:(n + 1) * 128],
                  rhs=CVB[:, 0:NS], start=True, stop=True)
            pg.op("pe", "matmul", rd=("CVB", "wig"), wr=(("ps", 3),), out=ps[3][:, 0:NS], lhsT=wig[:, n * 128:(n + 1) * 128],
                  rhs=CVB[:, 0:NS], start=True, stop=True)
            pg.op("act", "activation", rd=(("ps", 2), "rp"), wr=("TA",), out=TA[:, 0:NS], in_=ps[2][:, 0:NS], func=AF.Sigmoid,
                  bias=rp[:, 40 + n:41 + n])
            pg.op("act", "activation", rd=(("ps", 3), "rp"), wr=("TB",), out=TB[:, 0:NS], in_=ps[3][:, 0:NS], func=AF.Sigmoid,
                  bias=rp[:, 48 + n:49 + n])
            pg.op("act", "activation", rd=("TA", "nsp"), wr=("TA",), out=TA[:, 0:NS], in_=TA[:, 0:NS], func=AF.Exp, scale=nsp[:, n:n + 1])
            pg.op("dve", "tensor_tensor", rd=("TA",), wr=("TC",), out=TC[:, 0:NS], in0=TA[:, 0:NS], in1=TA[:, 0:NS], op=ALU.mult)
            pg.op("act", "activation", rd=("TC",), wr=("TC",), out=TC[:, 0:NS], in_=TC[:, 0:NS], func=AF.Sqrt, scale=-1.0, bias=1.0)
            pg.op("dve", "tensor_tensor", rd=("TB", "CV"), wr=("TB",), out=TB[:, 0:NS], in0=TB[:, 0:NS], in1=CV[:, 0:NS], op=ALU.mult)
            pg.op("dve", "tensor_tensor", rd=("TB", "TC"), wr=("TB",), out=TB[:, 0:NS], in0=TB[:, 0:NS], in1=TC[:, 0:NS], op=ALU.mult)
            pg.op("dve", "tensor_tensor", rd=("TA", "H0S"), wr=("HS",), out=HS[:, 0:NS], in0=TA[:, 0:NS], in1=H0S[:, n * NS:(n + 1) * NS], op=ALU.mult)
            pg.op("dve", "tensor_tensor", rd=("HS", "TB"), wr=("HS",), out=HS[:, 0:NS], in0=HS[:, 0:NS], in1=TB[:, 0:NS], op=ALU.add)
            pg.op("dve", "tensor_copy", rd=("HS",), wr=("HNS",), out=HNS[:, n * NS:(n + 1) * NS], in_=HS[:, 0:NS])
            gelu_mul(n, NS, YRS[:, n, :], (("YRS", n),), rnnT[:, n, T:T + NS], ("rnnT", n))
        for n in range(RC):
            bank = 0 if n < 4 else 1
            pg.op("pe", "transpose", rd=("HNS", "cident"), wr=(("ps", bank),), out=ps[bank][0:NS, (n % 4) * 128:(n % 4 + 1) * 128],
                  in_=HNS[:, n * NS:(n + 1) * NS], identity=cident[:, :])
        for k, bank in enumerate([0, 1]):
            pg.op("act", "activation", rd=(("ps", bank),), wr=("JNK",), out=SJ[0:NS, k * 512:(k + 1) * 512], in_=ps[bank][0:NS, 0:512], func=AF.Copy)
        pg.dma("sp", "s_ol", K=4, rd=("JNK",), out=h_smp[:, :], in_=SJ[0:NS, :])
        for n in range(RC):
            bank = 2 if n < 4 else 3
            pg.op("pe", "transpose", rd=(("XRS", n), "cident"), wr=(("ps", bank),), out=ps[bank][0:NS, (n % 4) * 128:(n % 4 + 1) * 128],
                  in_=XRS[:, n, 3 * NS:4 * NS], identity=cident[:, :])
        for k, bank in enumerate([2, 3]):
            pg.op("act", "activation", rd=(("ps", bank),), wr=("JNK",), out=SJ[0:NS, k * 512:(k + 1) * 512], in_=ps[bank][0:NS, 0:512], func=AF.Copy)
        pg.dma("sp", "s_ol", K=4, rd=("JNK",), out=conv_smp[:, 2048:3072], in_=SJ[0:NS, :])
        pg.dma("sp", "s_ol", K=4, rd=(), out=conv_smp[:, 0:2048], in_=state_conv[:, 1024:3072])

    pg.cut("P2d")
    pg.barrier()
    CS = aview(O_RT, 4096, F32)
    R1 = aview(O_RT + 4096, 4096, F32)
    HI = aview(O_RT + 8192, 2048, BF16)
    LO = aview(O_RT + 10240, 2048, BF16)
    for t in range(NT):
        bank = 0 if t < 4 else 1
        pg.op("pe", "transpose", rd=("CTk", "cident"), wr=(("ps", bank),), out=ps[bank][0:H, (t % 4) * 128:(t % 4) * 128 + 128],
              in_=CTk[:, NT + t, :], identity=cident[:, :])
    for hb in range(2):
        pg.op("dve", "tensor_scalar", rd=(("ps", hb),), wr=("CS",), out=CS[0:H, hb * 512:(hb + 1) * 512], in0=ps[hb][0:H, 0:512],
              scalar1=SQD, scalar2=None, op0=ALU.mult)
    pg.op("dve", "tensor_copy", rd=("CS",), wr=("C3",), out=C3[0:H, :], in_=CS[0:H, :])
    pg.op("dve", "tensor_tensor", rd=("CS", "C3"), wr=("R1",), out=R1[0:H, :], in0=CS[0:H, :], in1=C3[0:H, :], op=ALU.subtract)
    pg.op("dve", "tensor_copy", rd=("R1",), wr=("HI",), out=HI[0:H, :], in_=R1[0:H, :])
    pg.dma("sp", "s_c3", K=2, rd=("HI",), wr=("C3",), out=C3[8:16, :], in_=HI[0:H, :])
    pg.op("dve", "tensor_tensor", rd=("R1", "HI"), wr=("CS",), out=CS[0:H, :], in0=R1[0:H, :], in1=HI[0:H, :], op=ALU.subtract)
    pg.op("dve", "tensor_copy", rd=("CS",), wr=("LO",), out=LO[0:H, :], in_=CS[0:H, :])
    pg.dma("sp", "s_c3", K=2, rd=("LO",), wr=("C3",), out=C3[16:24, :], in_=LO[0:H, :])
    pg.barrier()

    pg.cut("P2")

    PT = [aview(O_RT + i * 1024, 1024, BF16) for i in range(3)]
    RD = aview(O_RT + 3072, 1024, F32)
    ptc = {"i": 0}
    sbank = {"i": 0}

    def attn_block(h, qt):
        q0 = qt * 512
        jobs = []
        for j in range(2 * NT):
            if j < NT:
                jobs.append((j, q0, False))
            else:
                k0 = (j - NT) * 128
                if k0 >= q0 + 512:
                    continue
                jobs.append((j, max(q0, k0), k0 >= q0))

        def emit_S(job):
            j, qlo, diag = job
            n = q0 + 512 - qlo
            bank = sbank["i"] % 2
            sbank["i"] += 1
            pg.op("pe", "matmul", rd=(("KT", h), ("QA", h)), wr=(("ps", bank),), out=ps[bank][:, 0:n], lhsT=KT[:, h, j * 128:(j + 1) * 128],
                  rhs=QA[:, h, qlo:qlo + n], start=True, stop=False)
            pg.op("pe", "matmul", rd=("sel3", "C3"), wr=(("ps", bank),), out=ps[bank][:, 0:n], lhsT=sel3[0:24, h * 128:(h + 1) * 128],
                  rhs=C3[0:24, qlo:qlo + n], start=False, stop=not diag)
            if diag:
                pg.op("pe", "matmul", rd=("identb", "ubias"), wr=(("ps", bank),), out=ps[bank][:, 0:128], lhsT=identb[:, :], rhs=ubias[:, :],
                      start=False, stop=True)
            pi = ptc["i"] % 3
            ptc["i"] += 1
            pg.op("act", "activation", rd=(("ps", bank), "BKb"), wr=(("PT", pi),), out=PT[pi][:, 0:n], in_=ps[bank][:, 0:n], func=AF.Exp,
                  bias=BKb[:, j, h:h + 1], scale=SCALE)
            return (j, qlo, n, pi)

        def emit_PV(info, first, last):
            j, qlo, n, pi = info
            o = qlo - q0
            pg.op("pe", "matmul", rd=(("V", j), ("PT", pi)), wr=(("ps", 2),), out=ps[2][:, o:o + n], lhsT=Vt[:, j, h * 128:(h + 1) * 128],
                  rhs=PT[pi][:, 0:n], start=first, stop=last)
            pg.op("pe", "matmul", rd=("onesb", ("PT", pi)), wr=(("ps", 3),), out=ps[3][:, o:o + n], lhsT=onesb[:, :], rhs=PT[pi][:, 0:n],
                  start=first, stop=last)
        infos = [emit_S(jobs[0])]
        for idx in range(len(jobs)):
            if idx + 1 < len(jobs):
                infos.append(emit_S(jobs[idx + 1]))
            emit_PV(infos[idx], idx == 0, idx == len(jobs) - 1)
        for hb in range(2):
            pg.op("dve", "reciprocal", rd=(("ps", 3),), wr=("RD",), out=RD[:, 0:256], in_=ps[3][:, hb * 256:(hb + 1) * 256])
            pg.op("dve", "tensor_tensor", rd=(("ps", 2), "RD"), wr=(("QA", h),), out=QA[:, h, q0 + hb * 256:q0 + (hb + 1) * 256],
                  in0=ps[2][:, hb * 256:(hb + 1) * 256], in1=RD[:, 0:256], op=ALU.mult)

    for h in range(H):
        for qt in range(2):
            attn_block(h, qt)
            pump(9)

    pg.cut("P3")

    SGa = [aview(O_RT + 4096 + i * 2048, 2048, F32) for i in range(3)]
    SGb = [aview(O_RT + 10240 + i * 2048, 2048, F32) for i in range(2)]
    QAall = tuple(("QA", h) for h in range(H))
    RTall = tuple(("rnnT", n) for n in range(RC))
    p4i = {"i": 0}
    GROUPS_P = GROUPS[0:2] if (with_samples and with_sattn) else GROUPS
    for c in range(KC):
        sa, sbk = SL["p4"][c]
        slot_a, (wga, woa) = slab_get(sa)
        for gi, (c0, n) in enumerate(GROUPS_P):
            b0 = psum_next(B4)
            mm_fm(b0, wga, 0, xnT, KC, c0, n, ("xnT",), slot_a)
            pg.op("act", "activation", rd=(("ps", b0),), wr=(("SGa", gi),), out=SGa[gi][:, 0:n], in_=ps[b0][:, 0:n], func=AF.Sigmoid)
            b2 = psum_next(B4)
            mm_fm(b2, woa, 0, QA, 8, c0, n, QAall, slot_a)
            pg.op("dve", "tensor_tensor", rd=(("ps", b2), ("SGa", gi)), wr=(("SGa", gi),), out=SGa[gi][:, 0:n], in0=ps[b2][:, 0:n],
                  in1=SGa[gi][:, 0:n], op=ALU.mult)
        if len(GROUPS_P) == 2:
            b0 = psum_next(B4)
            mm_fm(b0, wga, 0, xnT, KC, T, NS, ("xnT",), slot_a)
            pg.op("act", "activation", rd=(("ps", b0),), wr=("GS",), out=GS[:, c * NS:(c + 1) * NS], in_=ps[b0][:, 0:NS], func=AF.Sigmoid)
        slot_b, (wgr, wor) = slab_get(sbk)
        for gi, (c0, n) in enumerate(GROUPS_P):
            i = p4i["i"] % 2
            p4i["i"] += 1
            b1 = psum_next(B4)
            mm_fm(b1, wgr, 0, xnT, KC, c0, n, ("xnT",), slot_b)
            pg.op("act", "activation", rd=(("ps", b1),), wr=(("SGb", i),), out=SGb[i][:, 0:n], in_=ps[b1][:, 0:n], func=AF.Sigmoid)
            b3 = psum_next(B4)
            mm_fm(b3, wor, 0, rnnT, 8, c0, n, RTall, slot_b)
            pg.op("dve", "tensor_tensor", rd=(("ps", b3), ("SGb", i)), wr=(("SGb", i),), out=SGb[i][:, 0:n], in0=ps[b3][:, 0:n],
                  in1=SGb[i][:, 0:n], op=ALU.mult)
            pg.op("dve", "tensor_tensor", rd=(("SGa", gi), ("SGb", i)), wr=("oT",), out=oT[:, c, c0:c0 + n], in0=SGa[gi][:, 0:n],
                  in1=SGb[i][:, 0:n], op=ALU.add)
        if len(GROUPS_P) == 2:
            b1 = psum_next(B4)
            mm_fm(b1, wgr, 0, xnT, KC, T, NS, ("xnT",), slot_b)
            pg.op("act", "activation", rd=(("ps", b1),), wr=("RRS",), out=RRS[:, c * NS:(c + 1) * NS], in_=ps[b1][:, 0:NS], func=AF.Sigmoid)
            b3 = psum_next(B4)
            mm_fm(b3, wor, 0, rnnT, 8, T, NS, RTall, slot_b)
            pg.op("dve", "tensor_tensor", rd=(("ps", b3), "RRS"), wr=("RRS",), out=RRS[:, c * NS:(c + 1) * NS], in0=ps[b3][:, 0:NS],
                  in1=RRS[:, c * NS:(c + 1) * NS], op=ALU.mult)
        pump(9)
        if 2 <= c < 10:
            kcw = c - 2
            pg.dma("pool", "s_wo", K=4, wr=(("V", 2 * kcw), ("V", 2 * kcw + 1)), out=WOUT[:, kcw, :], in_=w_out[kcw * 128:(kcw + 1) * 128, :])
    pump(10 ** 6)
    if len(GROUPS_P) == 2:
        for s4 in range(4):
            slot, (wv,) = slab_get(SL["p4b"][s4])
            for cc in range(4):
                c = s4 * 4 + cc
                bank = psum_next(B4)
                mm_fm(bank, wv, cc, QA, 8, T, NS, QAall, slot)
                pg.op("dve", "tensor_tensor", rd=(("ps", bank), "GS"), wr=("GS",), out=GS[:, c * NS:(c + 1) * NS], in0=ps[bank][:, 0:NS],
                      in1=GS[:, c * NS:(c + 1) * NS], op=ALU.mult)
                pg.op("dve", "tensor_tensor", rd=("GS", "RRS"), wr=("oT",), out=oT[:, c, T:T + NS], in0=GS[:, c * NS:(c + 1) * NS],
                      in1=RRS[:, c * NS:(c + 1) * NS], op=ALU.add)
    else:
        for s4 in range(4):
            slab_get(SL["p4b"][s4])

    pg.cut("P4")
    pg.barrier()
    for kc in range(8, KC):
        pg.dma("pool", "s_wo", K=4, wr=("WOUT",), out=WOUT[:, kc, :], in_=w_out[kc * 128:(kc + 1) * 128, :])
    pg.dma("sp", "s_g", wr=("GBC",), out=GBC[:, :], in_=g_post_mix[0:1, :].partition_broadcast(P))
    hnT = xnT
    B6 = [0, 1, 2, 3, 4, 5]
    for t in range(NT + 1):
        m = 128 if t < NT else NS
        t0 = t * 128
        xs, hs_ = XT[0], XT[1]
        src = x_own[t0:t0 + 128, :] if t < NT else x_smp[:, :]
        pg.dma("sp", "s_x0", wr=(("XT", 0),), out=xs[0:m, :], in_=src)
        banks = []
        for cg in range(4):
            bank = psum_next(B6)
            banks.append(bank)
            for kc in range(KC):
                pg.op("pe", "matmul", rd=("oT", "WOUT", ("V", 2 * kc), ("V", 2 * kc + 1)) if kc < 8 else ("oT", "WOUT"), wr=(("ps", bank),), out=ps[bank][0:m, 0:512], lhsT=oT[:, kc, t0:t0 + m],
                      rhs=WOUT[:, kc, cg * 512:(cg + 1) * 512], start=(kc == 0), stop=(kc == KC - 1))
        rs, sr = rstd_from([ps[b][0:m, 0:512] for b in banks], m, 2, tuple(("ps", b) for b in banks), nparts=4)
        for cg in range(4):
            b = banks[cg]
            pg.op("dve", "scalar_tensor_tensor", rd=(("ps", b), sr, "GBC", ("XT", 1)), wr=(("XT", 1),), out=hs_[0:m, cg * 512:(cg + 1) * 512],
                  in0=ps[b][0:m, 0:512], scalar=rs, in1=GBC[0:m, cg * 512:(cg + 1) * 512], op0=ALU.mult, op1=ALU.mult)
        pg.op("dve", "tensor_tensor", rd=(("XT", 0), ("XT", 1)), wr=(("XT", 1),), out=hs_[0:m, :], in0=hs_[0:m, :], in1=xs[0:m, :], op=ALU.add)
        dst = y_own[t0:t0 + 128, :] if t < NT else y_smp[:, :]
        pg.dma("sp", "s_h", K=2, rd=(("XT", 1),), wr=(("hd", t),), out=dst, in_=hs_[0:m, :])
        norm_transpose(hs_[0:m, :], ("XT", 1), m, 3, hnT, t0, 80, "hnT")

    pg.cut("P5")
    pg.barrier()
    SGt = [aview(O_XT + i * 2048, 2048, F32) for i in range(2)]
    B8 = [0, 1, 2, 3, 4, 5, 6, 7]
    p6i = {"i": 0}
    for fc in range(FC):
        slot, (wg, wu) = slab_get(SL["p6"][fc])
        for (c0, n) in GROUPS:
            i = p6i["i"] % 2
            p6i["i"] += 1
            bg = psum_next(B8)
            mm_fm(bg, wg, 0, hnT, KC, c0, n, ("hnT",), slot)
            pg.op("act", "activation", rd=(("ps", bg),), wr=(("SGt", i),), out=SGt[i][:, 0:n], in_=ps[bg][:, 0:n], func=AF.Silu)
            bu = psum_next(B8)
            mm_fm(bu, wu, 0, hnT, KC, c0, n, ("hnT",), slot)
            pg.op("dve", "tensor_tensor", rd=(("ps", bu), ("SGt", i)), wr=("actT",), out=actT[:, fc, c0:c0 + n], in0=ps[bu][:, 0:n],
                  in1=SGt[i][:, 0:n], op=ALU.mult)

    pg.cut("P6")
    pg.barrier()
    FTs = [aview(O_OST, 2048, F32), aview(O_SST, 2048, F32)]
    p7i = {"i": 0}
    for c in range(KC):
        sa, sbk = SL["p7"][c]
        slot_a, (wa,) = slab_get(sa)
        gb = []
        for (c0, n) in GROUPS:
            bank = psum_next([0, 1, 2, 3, 4, 5])
            gb.append(bank)
            mm_fm(bank, wa, 0, actT, 22, c0, n, ("actT",), slot_a, first=True, last=False, kbase=0)
        slot_b, (wb_,) = slab_get(sbk)
        for gi, (c0, n) in enumerate(GROUPS):
            bank = gb[gi]
            mm_fm(bank, wb_, 0, actT, 22, c0, n, ("actT",), slot_b, first=False, last=True, kbase=22)
            i = p7i["i"] % 2
            p7i["i"] += 1
            evac("act", FTs[i][:, 0:n], ps[bank][:, 0:n], rd=(("ps", bank),), wr=(("FTs", i),))
            tb = 6 + (i % 2)
            ntile = (n + 127) // 128
            for k in range(ntile):
                mcols = min(128, n - k * 128)
                pg.op("pe", "transpose", rd=(("FTs", i), "cident"), wr=(("ps", tb),), out=ps[tb][0:mcols, k * 128:(k + 1) * 128],
                      in_=FTs[i][:, k * 128:k * 128 + mcols], identity=cident[:, :])
            for k in range(ntile):
                mcols = min(128, n - k * 128)
                t = c0 // 128 + k
                evac(alt(), FFt[t][0:mcols, c * 128:(c + 1) * 128], ps[tb][0:mcols, k * 128:(k + 1) * 128], rd=(("ps", tb),), wr=(("FF", t),))
    pg.dma("sp", "s_g", wr=(("wb", 0),), out=GBC7, in_=g_post_ffn[0:1, :].partition_broadcast(P))
    for t in range(NT + 1):
        m = 128 if t < NT else NS
        t0 = t * 128
        rs, sr = rstd_from([FFt[t][0:m, :]], m, t % 2, (("FF", t),))
        pg.op("dve", "scalar_tensor_tensor", rd=(("FF", t), sr, ("wb", 0)), wr=(("FF", t),), out=FFt[t][0:m, :], in0=FFt[t][0:m, :], scalar=rs,
              in1=GBC7[0:m, :], op0=ALU.mult, op1=ALU.mult)
        hsrc = y_own[t0:t0 + 128, :] if t < NT else y_smp[:, :]
        pg.dma("sp", "s_h7", rd=(("hd", t),), wr=(("wb", 1),), out=HST7[0:m, :], in_=hsrc)
        pg.op("dve", "tensor_tensor", rd=(("FF", t), ("wb", 1)), wr=(("FF", t),), out=FFt[t][0:m, :], in0=FFt[t][0:m, :], in1=HST7[0:m, :], op=ALU.add)
        pg.dma("sp", "s_y", K=4, rd=(("FF", t),), wr=(("hd", t),), out=hsrc, in_=FFt[t][0:m, :])

    pg.enabled = True
    pg.final_wait()
    print("program ops:", {e: len(pg.ops[e]) for e in ENGS}, "sems:", len(pg.sem_names()))
    sems = {name: es.enter_context(nc.semaphore(name)) for name in pg.sem_names()}
    with nc.Block() as block:
        pg.emit(block, sems)
    es.close()
    return nc


def _consts():
    c = np.zeros((P, 6 * 128), np.float32)
    c[:, 0:128] = np.eye(128, dtype=np.float32)
    i = np.arange(128)
    c[:, 128:256] = (i[:, None] <= i[None, :]).astype(np.float32)
    c[:, 256:384] = (i[:, None] > i[None, :]).astype(np.float32)
    c[:, 384:512] = np.where(i[:, None] > i[None, :], NEG, 0.0)
    c[:, 512:640] = 1.0
    sel3 = np.zeros((24, H, 128), np.float32)
    for r in range(24):
        sel3[r, r % 8, :] = 1.0
    seg = np.zeros((P, 2 * H * 17), np.float32)
    sm = np.ones((H, 16), np.float32); sm[:, 0] = 0.0
    seg[:, 0:128] = sm.reshape(1, -1)
    seg[1:, 136:144] = NEG
    bm = np.zeros((H, H, HD), np.float32)
    for h in range(H):
        bm[h, h, :] = 1.0
    return c, sel3.reshape(24, H * 128), seg, bm.reshape(H, H * HD)


_CACHE = {}


def _get_program(n_phys, **kw):
    key = (n_phys, tuple(sorted(kw.items())))
    if key not in _CACHE:
        _CACHE[key] = build_program(n_phys, **kw)
    return _CACHE[key]


def kernel(x_prompt, x_sample, cache_k, cache_v, cache_logf, state_h, state_conv, page_table,
           g_pre_mix, w_in, b_f, conv_w, conv_b, w_rg, b_rg, w_ig, b_ig, lru_lambda,
           w_o_attn, w_o_rnn, w_out, g_post_mix, g_pre_ffn, w_gate, w_up, w_down, g_post_ffn, _cores=None, _opts=None):
    f32 = lambda a: np.ascontiguousarray(np.asarray(a), dtype=np.float32)
    x_prompt = f32(x_prompt); x_sample = f32(x_sample)
    n_phys = int(np.asarray(cache_k).shape[1])
    B, S = x_prompt.shape[0], x_prompt.shape[1]
    xf = x_prompt.reshape(B * S, D)
    xs = x_sample.reshape(-1, D)
    ckv = np.concatenate([f32(cache_k).reshape(n_phys * 128, H * HD), f32(cache_v).reshape(n_phys * 128, H * HD)], axis=1)
    cl = f32(cache_logf).reshape(n_phys, 128 * H)
    sh = f32(state_h).reshape(-1, R)
    sc = f32(state_conv).reshape(-1, 3 * R)
    pt = np.ascontiguousarray(np.asarray(page_table), dtype=np.int32)
    cst, sel3, seg, bm = _consts()
    rnnp = np.concatenate([f32(conv_w).reshape(4 * RC, 128), f32(conv_b).reshape(RC, 128), f32(b_rg).reshape(RC, 128),
                           f32(b_ig).reshape(RC, 128), f32(lru_lambda).reshape(RC, 128)], axis=0)
    shared = dict(
        cst=cst, sel3=sel3, segm=seg, bmask=bm, iota=np.arange(P, dtype=np.float32).reshape(P, 1),
        cache_kv=ckv, cache_lf=cl,
        g_pre_mix=f32(g_pre_mix).reshape(KC, 128), g_pre_ffn=f32(g_pre_ffn).reshape(KC, 128),
        g_post_mix=f32(g_post_mix).reshape(1, D), g_post_ffn=f32(g_post_ffn).reshape(1, D),
        w_in=f32(w_in).reshape(D, IN_COLS), b_f=f32(b_f).reshape(1, H), rnnp=rnnp,
        w_rg=f32(w_rg).reshape(RC, 128, 128), w_ig=f32(w_ig).reshape(RC, 128, 128),
        w_o_attn=f32(w_o_attn).reshape(H * HD, D), w_o_rnn=f32(w_o_rnn).reshape(R, D), w_out=f32(w_out).reshape(D, D),
        w_gate=f32(w_gate).reshape(D, F), w_up=f32(w_up).reshape(D, F), w_down=f32(w_down).reshape(F, D),
    )
    cores = list(range(N_CORES)) if _cores is None else list(_cores)
    in_maps = []
    for c in cores:
        own = xf[c * T:(c + 1) * T]
        odd = (c % 2) == 1
        pre = xf[(c - 1) * T:c * T] if odd else own
        fl = np.zeros((P, 2), np.float32)
        fl[:, 0] = 1.0 if odd else 0.0
        fl[:, 1] = 0.0 if odd else NEG
        ptc = pt[c * NS:(c + 1) * NS]
        m = dict(shared)
        m.update(x_own=own, x_pre=pre, x_smp=xs[c * NS:(c + 1) * NS], flag=fl,
                 state_h=sh[c * NS:(c + 1) * NS], state_conv=sc[c * NS:(c + 1) * NS],
                 ptab=np.ascontiguousarray(ptc.reshape(1, NS * PAGES)), ptab16=np.ascontiguousarray(ptc.T))
        in_maps.append(m)
    nc = _get_program(n_phys, **(_opts or {}))
    res = run_bass_kernel_spmd(nc, in_maps, core_ids=list(range(len(cores)))).results
    nco = len(cores)
    cat = lambda k: np.concatenate([res[i][k] for i in range(nco)], axis=0)
    nb = max(1, nco // 2)
    y_p = cat("y_own").reshape(-1, S, D) if nco % 2 == 0 else cat("y_own")
    y_s = cat("y_smp").reshape(-1, 1, D)
    k_p = cat("k_own").reshape(1, -1, S, H, HD) if nco % 2 == 0 else cat("k_own")
    v_p = cat("v_own").reshape(1, -1, S, H, HD) if nco % 2 == 0 else cat("v_own")
    lf_p = cat("lf_own").reshape(1, -1, S, H) if nco % 2 == 0 else cat("lf_own")
    odd_i = [i for i, c in enumerate(cores) if c % 2 == 1]
    h_p = np.stack([res[i]["h_own"].reshape(R) for i in odd_i])[None] if odd_i else None
    c_p = np.stack([res[i]["conv_own"].reshape(RC, 3, 128).transpose(1, 0, 2).reshape(3, R) for i in odd_i])[None] if odd_i else None
    k_s = cat("k_smp").reshape(1, -1, 1, H, HD)
    v_s = cat("v_smp").reshape(1, -1, 1, H, HD)
    lf_s = cat("lf_smp").reshape(1, -1, 1, H)
    h_s = cat("h_smp").reshape(1, -1, R)
    c_s = cat("conv_smp").reshape(1, -1, 3, R)
    return (y_p, y_s, k_p, v_p, lf_p, h_p, c_p, k_s, v_s, lf_s, h_s, c_s)
```
